# Optimizing a Trainium2 kernel written in Bass

```python
import jax, jax.numpy as jnp
from jax import lax
import numpy as np

D_MODEL = 1024
BATCH = 8
SEQ = 2048
DEPTH = 2

GRID_W = 64
CTX_LEN = 256
D_MIX = D_MODEL
HEAD_DIM = 64
ATT_W = D_MIX // 2
N_HEADS = ATT_W // HEAD_DIM
N_KV_HEADS = 2
GQA_GROUP = N_HEADS // N_KV_HEADS
KV_W = N_KV_HEADS * HEAD_DIM
CONV_W = D_MIX // 4
FOUR_W = D_MIX - ATT_W - CONV_W
FOUR_HEADS = 4
FOUR_HEAD_DIM = FOUR_W // FOUR_HEADS
CONV_K = 31
WINDOW = 128
BLOCK = 128
ROPE_BASE = 10000.0
EPS = 1e-6
NEG_INF = -1e30
SPLITS = (ATT_W, KV_W, KV_W, ATT_W, CONV_W, CONV_W, CONV_W, FOUR_W, FOUR_W)
IN_W = ATT_W + 2 * KV_W + ATT_W + 3 * CONV_W + 2 * FOUR_W

kernel_name = 'hybrid_conv_fourier_swa_diffusion_block'


def rms_norm(x, g):
    xf = x.astype(jnp.float32)
    y = xf * lax.rsqrt(jnp.mean(xf * xf, axis=-1, keepdims=True) + EPS)
    return (y * g.astype(jnp.float32)).astype(x.dtype)


def layer_norm(x, g, b):
    xf = x.astype(jnp.float32)
    mu = jnp.mean(xf, axis=-1, keepdims=True)
    var = jnp.mean(jnp.square(xf - mu), axis=-1, keepdims=True)
    y = (xf - mu) * lax.rsqrt(var + EPS)
    return (y * g.astype(jnp.float32) + b.astype(jnp.float32)).astype(x.dtype)


def split_cols(p):
    outs = []
    off = 0
    for w in SPLITS:
        outs.append(p[..., off:off + w])
        off += w
    return outs


def rope_axis(x, pos):
    half = x.shape[-1] // 2
    freqs = ROPE_BASE ** (-jnp.arange(half, dtype=jnp.float32) / half)
    ang = pos[:, None] * freqs[None, :]
    cos = jnp.cos(ang)[:, None, :].astype(x.dtype)
    sin = jnp.sin(ang)[:, None, :].astype(x.dtype)
    x1, x2 = x[..., :half], x[..., half:]
    return jnp.concatenate([x1 * cos - x2 * sin, x1 * sin + x2 * cos], axis=-1)


def rope_2d(x, row, col):
    h = x.shape[-1] // 2
    return jnp.concatenate([rope_axis(x[..., :h], row), rope_axis(x[..., h:], col)], axis=-1)


def windowed_attention(q, k, v, kc, vc, sink):
    bn, s = q.shape[0], q.shape[1]
    nb = s // BLOCK
    n_ctx = kc.shape[1]
    scale = HEAD_DIM ** -0.5
    qb = q.reshape(bn, nb, BLOCK, N_KV_HEADS, GQA_GROUP, HEAD_DIM)
    pad = ((0, 0), (BLOCK, BLOCK), (0, 0), (0, 0))
    kp = jnp.pad(k, pad).reshape(bn, nb + 2, BLOCK, N_KV_HEADS, HEAD_DIM)
    vp = jnp.pad(v, pad).reshape(bn, nb + 2, BLOCK, N_KV_HEADS, HEAD_DIM)
    kw = jnp.concatenate([kp[:, :-2], kp[:, 1:-1], kp[:, 2:]], axis=2)
    vw = jnp.concatenate([vp[:, :-2], vp[:, 1:-1], vp[:, 2:]], axis=2)
    s_loc = jnp.einsum('bnqhgd,bnkhd->bnhgqk', qb, kw).astype(jnp.float32) * scale
    q_rel = jnp.arange(BLOCK) + BLOCK
    k_rel = jnp.arange(3 * BLOCK)
    band = jnp.abs(q_rel[:, None] - k_rel[None, :]) <= WINDOW
    k_abs = jnp.arange(nb)[:, None] * BLOCK - BLOCK + k_rel[None, :]
    valid = (k_abs >= 0) & (k_abs < s)
    mask = band[None, :, :] & valid[:, None, :]
    s_loc = jnp.where(mask[None, :, None, None], s_loc, NEG_INF)
    s_ctx = jnp.einsum('bnqhgd,blhd->bnhgql', qb, kc).astype(jnp.float32) * scale
    s_sink = jnp.broadcast_to(
        sink.astype(jnp.float32).reshape(1, 1, N_KV_HEADS, GQA_GROUP, 1, 1),
        s_loc.shape[:-1] + (1,))
    p = jax.nn.softmax(jnp.concatenate([s_loc, s_ctx, s_sink], axis=-1), axis=-1)
    p_loc = p[..., :3 * BLOCK].astype(v.dtype)
    p_ctx = p[..., 3 * BLOCK:3 * BLOCK + n_ctx].astype(v.dtype)
    o = (jnp.einsum('bnhgqk,bnkhd->bnqhgd', p_loc, vw)
         + jnp.einsum('bnhgql,blhd->bnqhgd', p_ctx, vc))
    return o.reshape(bn, s, ATT_W)


def context_attention(q, k, v, sink):
    bn, n_ctx = q.shape[0], q.shape[1]
    scale = HEAD_DIM ** -0.5
    qh = q.reshape(bn, n_ctx, N_KV_HEADS, GQA_GROUP, HEAD_DIM)
    sc = jnp.einsum('blhgd,bmhd->bhglm', qh, k).astype(jnp.float32) * scale
    s_sink = jnp.broadcast_to(
        sink.astype(jnp.float32).reshape(1, N_KV_HEADS, GQA_GROUP, 1, 1), sc.shape[:-1] + (1,))
    p = jax.nn.softmax(jnp.concatenate([sc, s_sink], axis=-1), axis=-1)
    o = jnp.einsum('bhglm,bmhd->blhgd', p[..., :n_ctx].astype(v.dtype), v)
    return o.reshape(bn, n_ctx, ATT_W)


def conformer_conv(a, b_glu, conv_w, conv_b, ln_g, ln_b):
    u = a * jax.nn.sigmoid(b_glu)
    y = lax.conv_general_dilated(
        u, conv_w[:, None, :], window_strides=(1,),
        padding=[(CONV_K // 2, CONV_K // 2)],
        dimension_numbers=('NWC', 'WIO', 'NWC'),
        feature_group_count=CONV_W) + conv_b
    return jax.nn.silu(layer_norm(y, ln_g, ln_b))


def fourier_mix(u, w_four, b_four):
    bn, n = u.shape[0], u.shape[1]
    uh = u.reshape(bn, n, FOUR_HEADS, FOUR_HEAD_DIM).astype(jnp.float32)
    f = jnp.fft.fftn(uh, axes=(1, 3), norm='ortho').real
    f = f.reshape(bn, n, FOUR_W).astype(u.dtype)
    return f @ w_four + b_four


def mixer_out(attn, parts, conv_w, conv_b, ln_g, ln_b, w_four, b_four, w_out):
    _, _, _, g_att, c_a, c_b, g_conv, f_u, g_four = parts
    y_conv = conformer_conv(c_a, c_b, conv_w, conv_b, ln_g, ln_b)
    y_four = fourier_mix(f_u, w_four, b_four)
    y = jnp.concatenate([attn * jax.nn.silu(g_att),
                         y_conv * jax.nn.silu(g_conv),
                         y_four * jax.nn.silu(g_four)], axis=-1)
    return y @ w_out


def setup_inputs(seed: int = 0) -> dict:
    key = jax.random.key(seed)
    ks = jax.random.split(key, 20)
    f32 = jnp.float32
    nrm = lambda k, shp, s: jax.random.normal(k, shp, f32) * s
    return {
        'x': nrm(ks[0], (BATCH, SEQ, D_MODEL), 1.0),
        'c': nrm(ks[1], (BATCH, D_MODEL), 1.0),
        'ctx': nrm(ks[2], (BATCH, CTX_LEN, D_MODEL), 1.0),
        'c_ctx': nrm(ks[3], (D_MODEL,), 1.0),
        'w_ada': nrm(ks[4], (DEPTH, D_MODEL, 3 * D_MODEL), 0.5 * D_MODEL ** -0.5),
        'b_ada': nrm(ks[5], (DEPTH, 3 * D_MODEL), 0.02),
        'norm_g': 1.0 + nrm(ks[6], (DEPTH, D_MODEL), 0.02),
        'w_in': nrm(ks[7], (DEPTH, D_MODEL, IN_W), D_MODEL ** -0.5),
        'attn_sink': nrm(ks[8], (DEPTH, N_HEADS), 0.5),
        'conv_w': nrm(ks[9], (DEPTH, CONV_K, CONV_W), CONV_K ** -0.5),
        'conv_b': nrm(ks[10], (DEPTH, CONV_W), 0.02),
        'conv_ln_g': 1.0 + nrm(ks[11], (DEPTH, CONV_W), 0.02),
        'conv_ln_b': nrm(ks[12], (DEPTH, CONV_W), 0.02),
        'w_four': nrm(ks[13], (DEPTH, FOUR_W, FOUR_W), FOUR_W ** -0.5),
        'b_four': nrm(ks[14], (DEPTH, FOUR_W), 0.02),
        'w_out': nrm(ks[15], (DEPTH, D_MIX, D_MODEL), D_MIX ** -0.5),
        'final_g': 1.0 + nrm(ks[16], (D_MODEL,), 0.02),
    }


def reference(x, c, ctx, c_ctx, w_ada, b_ada, norm_g, w_in, attn_sink, conv_w, conv_b,
              conv_ln_g, conv_ln_b, w_four, b_four, w_out, final_g):
    bn, s, _ = x.shape
    n_ctx = ctx.shape[1]
    ROWS = s // GRID_W
    row_pos = jnp.repeat(jnp.arange(ROWS, dtype=jnp.float32), GRID_W)
    col_pos = jnp.tile(jnp.arange(GRID_W, dtype=jnp.float32), ROWS)
    sc = jax.nn.silu(c)
    scc = jax.nn.silu(c_ctx)
    h_ctx = ctx
    for l in range(DEPTH):
        shift, scale, gate = jnp.split(sc @ w_ada[l] + b_ada[l], 3, axis=-1)
        shift_c, scale_c, gate_c = jnp.split(scc @ w_ada[l] + b_ada[l], 3, axis=-1)
        hc = rms_norm(h_ctx, norm_g[l]) * (1.0 + scale_c) + shift_c
        if l < DEPTH - 1:
            pc = split_cols(hc @ w_in[l])
            kc_raw, vc_raw = pc[1], pc[2]
        else:
            kv = hc @ w_in[l][:, ATT_W:ATT_W + 2 * KV_W]
            kc_raw, vc_raw = kv[..., :KV_W], kv[..., KV_W:]
        kc = kc_raw.reshape(bn, n_ctx, N_KV_HEADS, HEAD_DIM)
        vc = vc_raw.reshape(bn, n_ctx, N_KV_HEADS, HEAD_DIM)

        hx = rms_norm(x, norm_g[l]) * (1.0 + scale[:, None, :]) + shift[:, None, :]
        px = split_cols(hx @ w_in[l])
        q = rope_2d(px[0].reshape(bn, s, N_HEADS, HEAD_DIM), row_pos, col_pos)
        k = rope_2d(px[1].reshape(bn, s, N_KV_HEADS, HEAD_DIM), row_pos, col_pos)
        v = px[2].reshape(bn, s, N_KV_HEADS, HEAD_DIM)
        attn = windowed_attention(q, k, v, kc, vc, attn_sink[l])
        y = mixer_out(attn, px, conv_w[l], conv_b[l], conv_ln_g[l], conv_ln_b[l],
                      w_four[l], b_four[l], w_out[l])

        if l < DEPTH - 1:
            qc = pc[0].reshape(bn, n_ctx, N_HEADS, HEAD_DIM)
            attn_c = context_attention(qc, kc, vc, attn_sink[l])
            yc = mixer_out(attn_c, pc, conv_w[l], conv_b[l], conv_ln_g[l], conv_ln_b[l],
                           w_four[l], b_four[l], w_out[l])
            h_ctx = h_ctx + gate_c * yc
        x = x + gate[:, None, :] * y
    return rms_norm(x, final_g)
```

```python
import math
from contextlib import ExitStack

import numpy as np
import ml_dtypes

import concourse.bass as bass
import concourse.mybir as mybir
from concourse.bass_utils import run_bass_kernel_spmd

F32 = mybir.dt.float32
BF16 = mybir.dt.bfloat16
AF = mybir.ActivationFunctionType
ALU = mybir.AluOpType
NPBF = ml_dtypes.bfloat16

_DTSIZE = {F32: 4, BF16: 2}

D_MODEL = 1024
SEQ = 2048
NCTX = 256
DEPTH = 2
EPS = 1e-6
IN_W = 2560
NCORES = 8


class Sched:
    ENGS = ("pe", "act", "dve", "pool")

    def __init__(self, nc, flagged=None):
        self.nc = nc
        self.emit = flagged is not None
        self.flagged = flagged if flagged is not None else set()
        self.used = set()
        self.nops = 0
        self.buckets = {}
        self.mloc = {}
        self.sig = {}
        self.openg = {}
        self.waited = {}
        self.waited_idx = {}
        self.sems = {}
        self.semcnt = {}
        self.dma_ring = []
        self.dma_ring_cnt = []
        self.dma_i = 0
        self.out_dmas = []
        self.stack = None
        self.BK = 2048
        self.log = None
        self.tag = ''

    def setup_sems(self, stack, ndma=24):
        if not self.emit:
            self.ndma = ndma
            return
        self.stack = stack
        for e in self.ENGS:
            self.sems[e] = stack.enter_context(self.nc.semaphore("sig_" + e))
            self.semcnt[e] = 0
        self.rings = {}
        for q, n in (("sp", ndma), ("pool", 12)):
            sems = [stack.enter_context(self.nc.semaphore("dma_%s%d" % (q, i))) for i in range(n)]
            self.rings[q] = [sems, [0] * n, 0]
        self.ndma = ndma

    def eng(self, e):
        nc = self.nc
        return {"pe": nc.tensor, "act": nc.scalar, "dve": nc.vector, "pool": nc.gpsimd,
                "sp": nc.sync}[e]

    def region(self, ap):
        t = ap.tensor
        name = t.name
        space = str(ap.space)
        if "SB" not in space and "PSUM" not in space:
            return None
        if name not in self.mloc:
            m = self.nc.lookup_mloc(t)
            base = m.addr + (m.bank * 2048 if "PSUM" in space else 0)
            assert base % 64 == 0, (name, base)
            self.mloc[name] = base
        base = self.mloc[name]
        pairs = ap.ap
        pstep, npart = pairs[0]
        off = ap.offset
        if pstep > 0:
            p0 = off // pstep
            foff = off % pstep
        else:
            p0, foff = 0, off
        elo = ehi = 0
        for st, cnt in pairs[1:]:
            if st < 0:
                elo += (cnt - 1) * st
            else:
                ehi += (cnt - 1) * st
        sz = _DTSIZE[ap.dtype]
        lo = (base + (foff + elo) * sz) // 64 * 64
        hi = -(-(base + (foff + ehi + 1) * sz) // 64) * 64
        if "PSUM" in space:
            return ("P", 0, 128, lo // 2048 * 2048, -(-hi // 2048) * 2048)
        return ("S", p0, p0 + npart, lo, hi)

    def _scan(self, reg, want_reads):
        sp, p0, p1, lo, hi = reg
        deps = set()
        for b in range(lo // self.BK, (hi - 1) // self.BK + 1):
            for rec in self.buckets.get((sp, b), ()):
                rp0, rp1, rlo, rhi, idx, isw = rec
                if rlo < hi and lo < rhi and rp0 < p1 and p0 < rp1:
                    if isw or want_reads:
                        deps.add(idx)
        return deps

    def _insert(self, reg, idx, isw, engkey):
        sp, p0, p1, lo, hi = reg
        for b in range(lo // self.BK, (hi - 1) // self.BK + 1):
            lst = self.buckets.setdefault((sp, b), [])
            blo, bhi = max(lo, b * self.BK), min(hi, (b + 1) * self.BK)
            new = []
            for rec in lst:
                rp0, rp1, rlo, rhi, ridx, risw = rec
                cl, ch = max(rlo, b * self.BK), min(rhi, (b + 1) * self.BK)
                covered = (p0 <= rp0 and rp1 <= p1 and blo <= cl and ch <= bhi)
                if isw and covered:
                    continue
                if (not isw) and (not risw) and covered and self.openg.get(ridx) == engkey \
                        and engkey in self.ENGS:
                    continue
                new.append(rec)
            new.append((p0, p1, lo, hi, idx, isw))
            self.buckets[(sp, b)] = new

    def op(self, eng, fn, reads=(), writes=(), dma=False, extra_deps=()):
        idx = self.nops
        self.nops += 1
        engkey = ("dma%d" % idx) if dma else eng
        self.openg[idx] = engkey
        rregs = [r for r in (self.region(a) for a in reads) if r is not None]
        wregs = [r for r in (self.region(a) for a in writes) if r is not None]
        wregs += [r for r in rregs if r[0] == "P"]
        rregs = [r for r in rregs if r[0] != "P"]
        deps = set(extra_deps)
        for r in rregs:
            deps |= self._scan(r, False)
        for w in wregs:
            deps |= self._scan(w, True)
        fdeps = set()
        for d in deps:
            de = self.openg[d]
            if de == engkey:
                if eng == "pe":
                    continue
            fdeps.add(d)
        for w in wregs:
            self._insert(w, idx, True, engkey)
        for r in rregs:
            self._insert(r, idx, False, engkey)
        wi = self.waited_idx.setdefault(eng, {})
        latest = {}
        for d in fdeps:
            ek = self.openg[d]
            if latest.get(ek, -1) < d:
                latest[ek] = d
        needed = []
        for ek, d in latest.items():
            if ek.startswith("dma") or wi.get(ek, -1) < d:
                needed.append(d)
                wi[ek] = d
        self.used |= set(needed)
        if self.log is not None:
            self.log.append((idx, eng, self.tag, sorted(needed)))
        if not self.emit:
            return idx
        E = self.eng(eng)
        wt = self.waited.setdefault(eng, {})
        need = {}
        for d in needed:
            sem, val = self.sig[d]
            k = id(sem)
            if need.get(k, (None, -1))[1] < val:
                need[k] = (sem, val)
        if dma:
            ring = self.rings[eng]
            slot = ring[2] % len(ring[0])
            ring[2] += 1
            dsem = ring[0][slot]
            prev = ring[1][slot]
            if prev > 0:
                k = id(dsem)
                if need.get(k, (None, -1))[1] < prev:
                    need[k] = (dsem, prev)
        for k, (wsem, wval) in need.items():
            if wt.get(k, -1) >= wval:
                continue
            E.wait_ge(wsem, wval)
            wt[k] = wval
        inst = fn()
        if dma:
            ring[1][slot] = prev + 16
            inst.then_inc(dsem, 16)
            self.sig[idx] = (dsem, prev + 16)
        elif idx in self.flagged:
            self.semcnt[eng] += 1
            inst.then_inc(self.sems[eng], 1)
            self.sig[idx] = (self.sems[eng], self.semcnt[eng])
        return idx

    def final_wait(self, idxs):
        self.used |= set(idxs)
        if not self.emit:
            return
        for d in idxs:
            sem, val = self.sig[d]
            self.nc.sync.wait_ge(sem, val)


class K:
    def __init__(self, nc, S):
        self.nc = nc
        self.S = S
        self.rr = 0

    def mm(self, out, lhsT, rhs, start=True, stop=True):
        nc = self.nc
        return self.S.op("pe", lambda: nc.tensor.matmul(out, lhsT, rhs, start=start, stop=stop),
                         reads=[lhsT, rhs], writes=[out])

    def tr(self, out, in_, ident):
        nc = self.nc
        return self.S.op("pe", lambda: nc.tensor.transpose(out, in_, ident),
                         reads=[in_, ident], writes=[out])

    def act(self, out, in_, func, bias=None, scale=None, accum_out=None):
        nc = self.nc
        kw = {}
        reads = [in_]
        writes = [out]
        if bias is not None:
            kw["bias"] = bias
            if not isinstance(bias, (int, float)):
                reads.append(bias)
        if scale is not None:
            kw["scale"] = scale
            if not isinstance(scale, (int, float)):
                reads.append(scale)
        if accum_out is not None:
            kw["accum_out"] = accum_out
            writes.append(accum_out)
        return self.S.op("act", lambda: nc.scalar.activation(out, in_, func, **kw),
                         reads=reads, writes=writes)

    def _v(self, eng):
        return self.nc.vector if eng == "dve" else self.nc.gpsimd

    def tt(self, eng, out, in0, in1, op):
        v = self._v(eng)
        return self.S.op(eng, lambda: v.tensor_tensor(out, in0, in1, op), reads=[in0, in1],
                         writes=[out])

    def ts(self, eng, out, in0, s1, s2=None, op0=ALU.mult, op1=None):
        v = self._v(eng)
        reads = [in0]
        if not isinstance(s1, (int, float)):
            reads.append(s1)
        if s2 is not None and not isinstance(s2, (int, float)):
            reads.append(s2)
        if op1 is None:
            return self.S.op(eng, lambda: v.tensor_scalar(out, in0, s1, None, op0), reads=reads,
                             writes=[out])
        return self.S.op(eng, lambda: v.tensor_scalar(out, in0, s1, s2, op0, op1), reads=reads,
                         writes=[out])

    def stt(self, out, in0, scalar, in1, op0, op1):
        v = self.nc.vector
        reads = [in0, in1]
        if not isinstance(scalar, (int, float)):
            reads.append(scalar)
        return self.S.op("dve", lambda: v.scalar_tensor_tensor(out, in0, scalar, in1, op0, op1),
                         reads=reads, writes=[out])

    def cp(self, eng, out, in_):
        if eng == "act":
            nc = self.nc
            return self.S.op("act", lambda: nc.scalar.copy(out, in_), reads=[in_], writes=[out])
        v = self._v(eng)
        return self.S.op(eng, lambda: v.tensor_copy(out, in_), reads=[in_], writes=[out])

    def recip(self, out, in_):
        v = self.nc.vector
        return self.S.op("dve", lambda: v.reciprocal(out, in_), reads=[in_], writes=[out])

    def memset(self, eng, out, val):
        v = self._v(eng)
        return self.S.op(eng, lambda: v.memset(out, val), reads=[], writes=[out])

    def dma(self, q, out, in_):
        E = self.S.eng(q)
        return self.S.op(q, lambda: E.dma_start(out=out, in_=in_), reads=[in_], writes=[out],
                         dma=True)


VR = 94
V_SHIFT, V_SCALE, V_NG, V_TAP, V_CB, V_LG, V_LB, V_BF, V_LG2, V_LB2 = 0, 8, 16, 24, 86, 88, 90, 92, 96, 98
CB_ID, CB_PROT, CB_ONES, CB_INVC, CB_SEL0, CB_SEL1, CB_NYQ = 0, 128, 256, 384, 512, 640, 768
CBW = 832
QPERM = np.array([(j * 4 + c) * 64 + d for c in range(4) for j in range(2) for d in range(64)])


def _consts():
    cb = np.zeros((128, CBW), np.float32)
    cb[:, CB_ID:CB_ID + 128] = np.eye(128)
    prot = np.zeros((128, 128), np.float32)
    for m in range(128):
        d = m % 64
        partner = d + 16 if (d % 32) < 16 else d - 16
        prot[(m // 64) * 64 + partner, m] = 1.0
    cb[:, CB_PROT:CB_PROT + 128] = prot
    cb[:, CB_ONES:CB_ONES + 128] = 1.0
    cb[:, CB_INVC:CB_INVC + 128] = 1.0 / 256.0
    cb[64:66, CB_SEL0:CB_SEL0 + 64] = 2.0
    cb[0:2, CB_SEL1 + 64:CB_SEL1 + 128] = 2.0
    cb[:, CB_NYQ] = ((-1.0) ** np.arange(128)) / np.sqrt(SEQ)
    kk = np.arange(128)[:, None]
    qq = np.arange(128)[None, :]
    mprev = np.where(qq <= kk, 0.0, -30000.0).astype(np.float32)
    mnext = np.where(kk <= qq, 0.0, -30000.0).astype(np.float32)
    masks = np.concatenate([np.tile(mprev, (1, 4)), np.tile(mnext, (1, 4))], axis=1)
    t = np.arange(SEQ)
    row = (t // 64).astype(np.float32)
    col = (t % 64).astype(np.float32)
    freqs = (np.float32(10000.0) ** (-np.arange(16, dtype=np.float32) / np.float32(16))).astype(np.float32)
    rope = np.zeros((128, 2, SEQ), np.float32)
    for p in range(128):
        d = p % 64
        pos = row if d < 32 else col
        i = d % 16
        ang = (pos * freqs[i]).astype(np.float32)
        rope[p, 0] = np.cos(ang)
        sgn = -1.0 if (d % 32) < 16 else 1.0
        rope[p, 1] = sgn * np.sin(ang)
    ch = np.arange(256)
    same = (ch[:, None] // 64) == (ch[None, :] // 64)
    angd = 2.0 * np.pi * ((ch[:, None] % 64) * (ch[None, :] % 64) % 64) / 64.0
    cd = np.where(same, np.cos(angd), 0.0) / 16.0
    sd = np.where(same, np.sin(angd), 0.0) / 16.0
    cdsd = np.concatenate([cd, sd], axis=1).reshape(2, 128, 512).transpose(1, 0, 2)

    def cs(n):
        a = np.arange(n)
        ang = 2.0 * np.pi * ((a[:, None] * a[None, :]) % n) / n
        return np.stack([np.cos(ang), -np.sin(ang)]) / np.sqrt(n)
    cs256 = cs(256).reshape(2, 2, 128, 256).transpose(2, 0, 1, 3)
    cs2048 = cs(2048)
    return dict(cb=cb.astype(NPBF), idf=np.eye(128, dtype=np.float32), masks=masks.astype(NPBF),
                rope=rope.astype(NPBF), cdsd=np.ascontiguousarray(cdsd).astype(NPBF),
                cs256=np.ascontiguousarray(cs256).astype(NPBF), cs2048=cs2048.astype(NPBF))


_CONST_CACHE = {}


def _get_consts():
    if not _CONST_CACHE:
        _CONST_CACHE.update(_consts())
    return _CONST_CACHE


def _host_inputs(x, c, ctx, c_ctx, w_ada, b_ada, norm_g, w_in, attn_sink, conv_w, conv_b,
                 conv_ln_g, conv_ln_b, w_four, b_four, w_out, final_g):
    f = np.float32
    w_in = np.asarray(w_in, f)
    w_out = np.asarray(w_out, f)
    w_in_p = np.concatenate([w_in[:, :, 0:512][:, :, QPERM], w_in[:, :, 512:768],
                             w_in[:, :, 768:1280][:, :, QPERM], w_in[:, :, 1280:]], axis=2)
    w_out_p = np.concatenate([w_out[:, 0:512][:, QPERM], w_out[:, 512:]], axis=1)
    vecs = np.zeros((DEPTH, VR, 128), f)
    for l in range(DEPTH):
        vecs[l, 0:16] = np.asarray(b_ada, f)[l, 0:2048].reshape(16, 128)
        vecs[l, 16:24] = np.asarray(norm_g, f)[l].reshape(8, 128)
        cw = np.asarray(conv_w, f)[l]
        vecs[l, 24:55] = cw[:, 0:128]
        vecs[l, 55:86] = cw[:, 128:256]
        vecs[l, 86:88] = np.asarray(conv_b, f)[l].reshape(2, 128)
        vecs[l, 88:90] = np.asarray(conv_ln_g, f)[l].reshape(2, 128)
        vecs[l, 90:92] = np.asarray(conv_ln_b, f)[l].reshape(2, 128)
        vecs[l, 92:94] = np.asarray(b_four, f)[l].reshape(2, 128)
    shared = dict(vecs=vecs, bgate=np.ascontiguousarray(np.asarray(b_ada, f)[:, 2048:3072]),
                  final_g=np.asarray(final_g, f).reshape(1, 1024),
                  sinkrep=np.ascontiguousarray(np.repeat(np.asarray(attn_sink, f).reshape(16), 128).reshape(1, 2048)),
                  w_ada=np.ascontiguousarray(np.asarray(w_ada, f)), w_in=np.ascontiguousarray(w_in_p),
                  w_out=np.ascontiguousarray(w_out_p), w_four=np.ascontiguousarray(np.asarray(w_four, f)))
    shared.update(_get_consts())
    maps = []
    cc = np.asarray(c_ctx, f).reshape(8, 128)
    for b in range(NCORES):
        cvec = np.zeros((16, 128), f)
        cvec[0::2] = np.asarray(c, f)[b].reshape(8, 128)
        cvec[1::2] = cc
        m = dict(shared)
        m["x"] = np.ascontiguousarray(np.asarray(x, f)[b])
        m["ctx"] = np.ascontiguousarray(np.asarray(ctx, f)[b])
        m["cvec"] = cvec
        maps.append(m)
    return maps


def build(flagged, dbg=(), nlayers=DEPTH, stop_at=99):
    nc = bass.Bass("TRN2", target_bir_lowering=False)
    S = Sched(nc, flagged)
    k = K(nc, S)

    def dt(name, shape, dtype, kind="ExternalInput"):
        return nc.dram_tensor(name, shape, dtype, kind=kind).ap()

    x_d = dt("x", [SEQ, D_MODEL], F32)
    ctx_d = dt("ctx", [NCTX, D_MODEL], F32)
    cvec_d = dt("cvec", [16, 128], F32)
    vecs_d = dt("vecs", [DEPTH, VR, 128], F32)
    bgate_d = dt("bgate", [DEPTH, 1024], F32)
    fg_d = dt("final_g", [1, 1024], F32)
    sink_d = dt("sinkrep", [1, 2048], F32)
    wada_d = dt("w_ada", [DEPTH, 1024, 3072], F32)
    win_d = dt("w_in", [DEPTH, 1024, IN_W], F32)
    wout_d = dt("w_out", [DEPTH, 1024, 1024], F32)
    wfour_d = dt("w_four", [DEPTH, 256, 256], F32)
    cb_d = dt("cb", [128, CBW], BF16)
    idf_d = dt("idf", [128, 128], F32)
    masks_d = dt("masks", [128, 1024], BF16)
    rope_d = dt("rope", [128, 2, SEQ], BF16)
    cdsd_d = dt("cdsd", [128, 2, 512], BF16)
    cs256_d = dt("cs256", [128, 2, 2, 256], BF16)
    cs2048_d = dt("cs2048", [2, SEQ, SEQ], BF16)
    out_d = dt("out", [SEQ, D_MODEL], F32, kind="ExternalOutput")
    out_dmas = []

    with ExitStack() as st:
        S.setup_sems(st)

        uniq = [0]

        def sb(name, shape, dtype, stack=st):
            uniq[0] += 1
            return stack.enter_context(nc.sbuf_tensor("s%d_%s" % (uniq[0], name), shape, dtype))

        PS = [st.enter_context(nc.psum_tensor("ps%d" % i, [128, 512], F32)) for i in range(8)]
        bank = [0]

        def nb():
            b = bank[0]
            bank[0] = (b + 1) % 8
            return b

        dumped = {}

        def dump(name, t, shape, dtype):
            if name in dbg:
                dumped[name] = dumped.get(name, 0) + 1
                suffix = "" if dumped[name] == 1 else "_L%d" % (dumped[name] - 1)
                d = dt("dbg_" + name + suffix, list(shape), dtype, kind="ExternalOutput")
                idx = [slice(None)] * len(shape)
                out_dmas.append(k.dma("sp", d[tuple(idx)], t[tuple(idx)]))

        X = sb("X", [128, 16, 1024], F32)
        HC = sb("HC", [128, 2, 1024], F32)
        hT = sb("hT", [128, 8, SEQ], BF16)
        hcT = sb("hcT", [128, 8, NCTX], BF16)
        yT = sb("yT", [128, 4, SEQ], BF16)
        ycT = sb("ycT", [128, 4, NCTX], BF16)
        CB = sb("CB", [128, CBW], BF16)
        vT = sb("vT", [128, DEPTH, 112], F32)
        scf = sb("scf", [128, 16], F32)
        scT = sb("scT", [128, 16, 2], BF16)
        ssX = sb("ssX", [128, 16], F32)
        ABm = sb("ABm", [128, DEPTH, 2, 2, 8], F32)

        ident = CB[:, CB_ID:CB_ID + 128]
        prot = CB[:, CB_PROT:CB_PROT + 128]
        ones = CB[:, CB_ONES:CB_ONES + 128]
        invc = CB[:, CB_INVC:CB_INVC + 128]

        k.dma("sp", CB[:, :], cb_d[:, :])
        xv = x_d.rearrange("(t p) d -> p t d", p=128)
        for i in range(4):
            k.dma("sp", X[:, 4 * i:4 * i + 4, :], xv[:, 4 * i:4 * i + 4, :])
        k.dma("sp", HC[:, :, :], ctx_d.rearrange("(t p) d -> p t d", p=128))
        with ExitStack() as s0:
            idf = sb("idf", [128, 128], F32, s0)
            vraw = sb("vraw", [128, 128], F32, s0)
            craw = sb("craw", [16, 128], F32, s0)
            cT = sb("cT", [128, 16], F32, s0)
            thc = sb("thc", [128, 16], F32, s0)
            sqj0 = sb("sqj0", [128, 1024], BF16, s0)
            for t in range(16):
                k.act(sqj0[:, :], X[:, t, :], AF.Square, accum_out=ssX[:, t:t + 1])
            k.dma("sp", idf[:, :], idf_d[:, :])
            k.dma("sp", craw[:, :], cvec_d[:, :])
            for l in range(DEPTH):
                k.dma("sp", vraw[0:VR, :], vecs_d[l])
                b = nb()
                k.tr(PS[b][:, 0:VR], vraw[0:VR, :], idf[0:VR, 0:VR])
                k.cp("dve", vT[:, l, 0:VR], PS[b][:, 0:VR])
                k.ts("dve", vT[:, l, V_TAP:V_TAP + 62], vT[:, l, V_TAP:V_TAP + 62], 0.5, None, ALU.mult)
                k.ts("dve", vT[:, l, V_LG:V_LG + 4], vT[:, l, V_LG:V_LG + 4], 0.25, None, ALU.mult)
                k.ts("dve", vT[:, l, V_BF:V_BF + 2], vT[:, l, V_BF:V_BF + 2], 0.5, None, ALU.mult)
                k.ts("dve", vT[:, l, V_LG2:V_LG2 + 4], vT[:, l, V_LG:V_LG + 4], 2.0, None, ALU.mult)
            b = nb()
            k.tr(PS[b][:, 0:16], craw[0:16, :], idf[0:16, 0:16])
            k.cp("dve", cT[:, :], PS[b][:, 0:16])
            k.act(thc[:, :], cT[:, :], AF.Tanh, scale=0.5)
            k.stt(scf[:, :], thc[:, :], 1.0, cT[:, :], ALU.add, ALU.mult)
            k.ts("dve", scT[:, 0:8, :], scf[:, :].rearrange("p (k s) -> p k s", s=2), 0.5, None, ALU.mult)

        def proj_fm(W, col0, hsrc, tok0, ntok, alloc=None):
            b = (alloc or nb)()
            for kk in range(8):
                k.mm(PS[b][:, 0:ntok], W[:, kk, col0:col0 + 128], hsrc[:, kk, tok0:tok0 + ntok],
                     start=(kk == 0), stop=(kk == 7))
            return PS[b][:, 0:ntok]

        def gate_chunk(ps_ap, dst_ap, tmp_ap):
            k.act(tmp_ap, ps_ap, AF.Tanh, scale=0.5)
            k.stt(dst_ap, tmp_ap, 1.0, ps_ap, ALU.add, ALU.mult)

        def load_w(dst, src_rows, c0, c1, nk=8):
            for kk in range(nk):
                k.dma("pool", dst[:, kk, :], src_rows[kk * 128:(kk + 1) * 128, c0:c1])

        def ada(l):
            with ExitStack() as sa:
                wad = [sb("wad%d" % i, [128, 8, 512], BF16, sa) for i in range(2)]
                modv = sb("modv", [128, 16, 2], F32, sa)
                tmp8 = sb("tmp8", [128, 16], F32, sa)
                bmod = nb()
                for blk in range(4):
                    wbuf = wad[blk % 2]
                    load_w(wbuf, wada_d[l], blk * 512, (blk + 1) * 512)
                    for fcl in range(4):
                        fc = blk * 4 + fcl
                        for kk in range(8):
                            k.mm(PS[bmod][:, fc * 2:fc * 2 + 2], wbuf[:, kk, fcl * 128:(fcl + 1) * 128],
                                 scT[:, kk, :], start=(kk == 0), stop=(kk == 7))
                psv = PS[bmod][:, 0:32].rearrange("p (f s) -> p f s", s=2)
                for s in range(2):
                    k.tt("dve", modv[:, :, s], psv[:, :, s], vT[:, l, 0:16], ALU.add)
                for s in range(2):
                    k.ts("dve", tmp8[:, 0:8], modv[:, 8:16, s], 1.0, None, ALU.add)
                    k.tt("dve", ABm[:, l, s, 0, :], tmp8[:, 0:8], vT[:, l, V_NG:V_NG + 8], ALU.mult)
                    k.cp("dve", ABm[:, l, s, 1, :], modv[:, 0:8, s])

        def gates(l, gate_bc, gate_c_bc):
            with ExitStack() as sa:
                wad = [sb("wadg%d" % i, [128, 8, 512], BF16, sa) for i in range(2)]
                sc_rep = sb("sc_rep", [128, 2, 8, 128], BF16, sa)
                bgb = sb("bgb", [128, 1024], F32, sa)
                k.dma("sp", bgb[:, :], bgate_d[l:l + 1, :].partition_broadcast(128))
                ns = 2 if l == 0 else 1
                for s in range(ns):
                    for kk in range(8):
                        k.ts("dve", sc_rep[:, s, kk, :], ones, scf[:, 2 * kk + s:2 * kk + s + 1], 0.5,
                             ALU.mult, ALU.mult)
                for nh in range(2):
                    wbuf = wad[nh]
                    load_w(wbuf, wada_d[l], 2048 + nh * 512, 2048 + (nh + 1) * 512)
                    for s in range(ns):
                        b = nb()
                        for kk in range(8):
                            k.mm(PS[b][:, 0:512], sc_rep[:, s, kk, :], wbuf[:, kk, :],
                                 start=(kk == 0), stop=(kk == 7))
                        dstg = gate_bc if s == 0 else gate_c_bc
                        k.tt("dve", dstg[:, nh * 512:(nh + 1) * 512], PS[b][:, 0:512],
                             bgb[:, nh * 512:(nh + 1) * 512], ALU.add)

        def phase_norm(l, Xb, T, s, dst):
            with ExitStack() as sn:
                ss = sb("ss", [128, 16], F32, sn)
                rstd = sb("rstd", [128, 16], F32, sn)
                junk = sb("junk", [128, 1024], BF16, sn)
                xn = [sb("xn%d" % i, [128, 1024], BF16, sn) for i in range(8)]
                if Xb is X:
                    ss = ssX
                else:
                    for t in range(T):
                        k.act(junk[:, :], Xb[:, t, :], AF.Square, accum_out=ss[:, t:t + 1])
                k.act(rstd[:, 0:T], ss[:, 0:T], AF.Sqrt, bias=EPS, scale=1.0 / D_MODEL)
                k.recip(rstd[:, 0:T], rstd[:, 0:T])
                G4 = min(4, T)
                for g0 in range(0, T, G4):
                    for i in range(G4):
                        t = g0 + i
                        xb = xn[t % 8]
                        if t % 2 == 0:
                            k.ts("dve", xb[:, :], Xb[:, t, :], rstd[:, t:t + 1], None, ALU.mult)
                        else:
                            k.act(xb[:, :], Xb[:, t, :], AF.Copy, scale=rstd[:, t:t + 1])
                    for c in range(8):
                        b = nb()
                        pv = PS[b][:, :].bitcast(BF16)
                        for i in range(G4):
                            k.tr(pv[:, i * 128:(i + 1) * 128], xn[(g0 + i) % 8][:, c * 128:(c + 1) * 128], ident)
                        o = dst[:, c, g0 * 128:(g0 + G4) * 128]
                        A = ABm[:, l, s, 0, c:c + 1]
                        B = ABm[:, l, s, 1, c:c + 1]
                        if c % 2 == 0:
                            k.act(o, pv[:, 0:G4 * 128], AF.Identity, bias=B, scale=A)
                        else:
                            k.ts("dve", o, pv[:, 0:G4 * 128], A, B, ALU.mult, ALU.add)

        def load_A_weights(l, stack):
            WKV = sb("WKV", [128, 8, 256], BF16, stack)
            WQG = sb("WQG", [128, 8, 1024], BF16, stack)
            load_w(WKV, win_d[l], 512, 768)
            load_w(WQG[:, :, 0:512], win_d[l], 0, 512)
            load_w(WQG[:, :, 512:1024], win_d[l], 768, 1280)
            return WKV, WQG

        def phase_A(l, WKV, WQG):
            with ExitStack() as sa:
                masks = sb("masks", [128, 1024], BF16, sa)
                rope_t = sb("rope_t", [128, 2, SEQ], BF16, sa)
                kT = sb("kT", [128, SEQ], BF16, sa)
                Vaug = sb("Vaug", [128, 16, 2, 128], BF16, sa)
                kcT = sb("kcT", [128, NCTX], BF16, sa)
                Vcaug = sb("Vcaug", [128, 2, 2, 128], BF16, sa)
                raw = sb("raw", [128, 512], BF16, sa)
                tA = sb("tA", [128, 512], BF16, sa)
                tB = sb("tB", [128, 512], BF16, sa)
                den = [sb("den%d" % j, [128, 512], BF16, sa) for j in range(2)]
                k.dma("sp", masks[:, :], masks_d[:, :])
                k.dma("sp", rope_t[:, :, :], rope_d[:, :, :])
                for V_ in (Vaug, Vcaug):
                    k.memset("pool", V_[:, :, :, :], 0.0)
                    k.memset("pool", V_[:, :, 0, 64:65], 1.0)
                    k.memset("pool", V_[:, :, 1, 0:1], 1.0)
                with ExitStack() as ss_:
                    sinkst = sb("sinkst", [128, 512], F32, ss_)
                    k.memset("dve", sinkst[:, :], 0.0)
                    for j in range(2):
                        k.memset("pool", den[j][:, :], 0.0)
                        r = 65 if j == 0 else 1
                        k.dma("sp", sinkst[r:r + 1, :], sink_d[0:1, l * 1024 + j * 512:l * 1024 + (j + 1) * 512])
                    k.act(den[0][64:66, :], sinkst[64:66, :], AF.Exp)
                    k.act(den[1][0:2, :], sinkst[0:2, :], AF.Exp)

                def v4(ap):
                    return ap.rearrange("p (a b) -> p a b", a=4)

                sbank = [0]

                def nbs():
                    b = sbank[0]
                    sbank[0] = (b + 1) % 3
                    return b

                def rope(ps_ap, dst_ap, tok0, n=512, split=False, alloc=nb):
                    k.cp("act", raw[:, 0:n], ps_ap)
                    b = alloc()
                    k.mm(PS[b][:, 0:n], prot, raw[:, 0:n])
                    k.tt("dve", tA[:, 0:n], ps_ap, rope_t[:, 0, tok0:tok0 + n], ALU.mult)
                    k.tt("dve", tB[:, 0:n], PS[b][:, 0:n], rope_t[:, 1, tok0:tok0 + n], ALU.mult)
                    if split:
                        k.tt("pool", dst_ap, v4(tA[:, 0:n]), v4(tB[:, 0:n]), ALU.add)
                    else:
                        k.tt("pool", dst_ap, tA[:, 0:n], tB[:, 0:n], ALU.add)

                with ExitStack() as skv:

                    def vproj(hsrc, t, V_):
                        b = nb()
                        for kk in range(8):
                            k.mm(PS[b][:, 0:128], hsrc[:, kk, t * 128:(t + 1) * 128], WKV[:, kk, 128:256],
                                 start=(kk == 0), stop=(kk == 7))
                        ve = "act" if t % 2 else "dve"
                        k.cp(ve, V_[:, t, 0, 0:64], PS[b][:, 0:64])
                        k.cp(ve, V_[:, t, 1, 64:128], PS[b][:, 64:128])

                    psk = proj_fm(WKV, 0, hcT, 0, NCTX)
                    k.cp("act", kcT[:, :], psk)
                    for t in range(2):
                        vproj(hcT, t, Vcaug)
                    for t in range(16):
                        vproj(hT, t, Vaug)
                    for g4 in range(4):
                        psk = proj_fm(WKV, 0, hT, g4 * 512, 512)
                        rope(psk, kT[:, g4 * 512:(g4 + 1) * 512], g4 * 512)
                dump("kT", kT, [128, SEQ], BF16)
                dump("Vaug", Vaug, [128, 16, 2, 128], BF16)

                qT = [sb("qT%d" % i, [128, 4, 4, 128], BF16, sa) for i in range(2)]
                G = [sb("G%d" % i, [128, 4, 4, 128], BF16, sa) for i in range(2)]
                thb = sb("thb", [128, 512], BF16, sa)
                P = [sb("P%d" % i, [128, 5, 512], BF16, sa) for i in range(2)]
                rb = sb("rb", [128, 512], F32, sa)
                tq = sb("tq", [128, 512], BF16, sa)
                cnt = [0]
                csrc = [(kcT, kb * 128, Vcaug, kb, None) for kb in range(2)]
                mprev = masks[:, 0:512]
                mnext = masks[:, 512:1024]

                def qg_chunks(buf, hsrc, tok0, n, do_rope):
                    nq = n // 128
                    chunks = []
                    for c in range(4):
                        def fq(c=c):
                            ps = proj_fm(WQG, c * 128, hsrc, tok0, n, nbs)
                            if do_rope:
                                rope(ps, qT[buf][:, :, c, :], tok0, n, split=True, alloc=nbs)
                            else:
                                k.cp("act", qT[buf][:, 0:nq, c, :], ps.rearrange("p (a b) -> p a b", a=nq))
                        chunks.append(fq)
                    for c in range(4):
                        def fg(c=c):
                            ps = proj_fm(WQG, 512 + c * 128, hsrc, tok0, n, nbs)
                            k.act(thb[:, 0:n], ps, AF.Tanh, scale=0.5)
                            k.stt(G[buf][:, 0:nq, c, :], thb[:, 0:n].rearrange("p (a b) -> p a b", a=nq), 1.0,
                                  ps.rearrange("p (a b) -> p a b", a=nq), ALU.add, ALU.mult)
                        chunks.append(fg)
                    return chunks

                def qg(buf, hsrc, tok0, n, do_rope):
                    for f in qg_chunks(buf, hsrc, tok0, n, do_rope):
                        f()

                units = []
                if l == 0:
                    for qb in range(2):
                        for j in range(2):
                            units.append((1, qb, j, csrc, ycT, qb * 128, None))
                for g4 in range(4):
                    for qi in range(4):
                        qb = g4 * 4 + qi
                        for j in range(2):
                            src = []
                            if qb > 0:
                                src.append((kT, (qb - 1) * 128, Vaug, qb - 1, mprev))
                            src.append((kT, qb * 128, Vaug, qb, None))
                            if qb < 15:
                                src.append((kT, (qb + 1) * 128, Vaug, qb + 1, mnext))
                            units.append((g4 % 2, qi, j, src + csrc, yT, qb * 128, g4 if (qi == 0 and j == 0) else None))
                state = {}

                def S1(u):
                    buf, qi, j, sources, dst, dtok, _ = units[u]
                    r0, r1 = j * 64, (j + 1) * 64
                    pb = P[u % 2]
                    for si, (kt, koff, va, vt, mask) in enumerate(sources):
                        b = nbs()
                        k.mm(PS[b][:, 0:512], kt[r0:r1, koff:koff + 128], qT[buf][r0:r1, qi, :, :],
                             start=True, stop=(mask is None))
                        if mask is not None:
                            k.mm(PS[b][:, 0:512], ident, mask, start=False, stop=True)
                        k.act(pb[:, si, :], PS[b][:, 0:512], AF.Exp, scale=0.125)

                def S2(u):
                    buf, qi, j, sources, dst, dtok, _ = units[u]
                    pb = P[u % 2]
                    bo = 3 + 2 * j + ((u // 2) % 2)
                    n = len(sources)
                    for si, (kt, koff, va, vt, mask) in enumerate(sources):
                        k.mm(PS[bo][:, 0:512], va[:, vt, j, :], pb[:, si, :], start=(si == 0), stop=(si == n - 1))
                    dp = 64 if j == 0 else 0
                    k.cp("act", den[j][dp:dp + 1, :], PS[bo][dp:dp + 1, 0:512])
                    state[u] = bo

                def S3(u0, u1):
                    bb = 7
                    k.mm(PS[bb][:, 0:512], CB[:, CB_SEL0:CB_SEL0 + 128], den[0][:, :], start=True, stop=False)
                    k.mm(PS[bb][:, 0:512], CB[:, CB_SEL1:CB_SEL1 + 128], den[1][:, :], start=False, stop=True)
                    k.recip(rb[:, :], PS[bb][:, 0:512])
                    for u in (u0, u1):
                        buf, qi, j, sources, dst, dtok, _ = units[u]
                        r0, r1 = j * 64, (j + 1) * 64
                        k.tt("dve", tq[r0:r1, :], PS[state[u]][r0:r1, 0:512], rb[r0:r1, :], ALU.mult)
                        k.tt("pool", dst[r0:r1, :, dtok:dtok + 128], v4(tq[r0:r1, :]), G[buf][r0:r1, qi, :, :], ALU.mult)

                if l == 0:
                    qg(1, hcT, 0, NCTX, False)
                qg(0, hT, 0, 512, True)
                S1(0)
                pending = []
                for u in range(len(units)):
                    g_start = units[u][6]
                    if g_start is not None and g_start + 1 < 4:
                        for f in pending:
                            f()
                        pending = qg_chunks((g_start + 1) % 2, hT, (g_start + 1) * 512, 512, True)
                    if u + 1 < len(units):
                        if units[u + 1][6] is not None:
                            for f in pending:
                                f()
                            pending = []
                        S1(u + 1)
                    S2(u)
                    if pending:
                        pending.pop(0)()
                    if units[u][2] == 1:
                        S3(u - 1, u)
                    if l == 0 and u == 3:
                        dump("ycT_A", ycT, [128, 4, NCTX], BF16)
                dump("yT_A", yT, [128, 4, SEQ], BF16)


        def outproj(l, half, gate_bc, gate_c_bc, prefetch=None, Wo_pre=None):
            with ExitStack() as so:
                Wo = Wo_pre if Wo_pre is not None else sb("Wo", [128, 4, 1024], BF16, so)
                Wg = sb("Wg", [128, 4, 1024], BF16, so)
                tmpc = sb("tmpc", [128, 512], F32, so)
                evb = [sb("evb%d" % i, [128, 512], F32, so) for i in range(2)]
                sqj = sb("sqj", [128, 1024], BF16, so)
                if Wo_pre is None:
                    load_w(Wo, wout_d[l, half * 512:(half + 1) * 512, :], 0, 1024, nk=4)
                if prefetch is not None:
                    prefetch()
                for c in range(4):
                    k.tt("dve" if c % 2 else "pool", Wg[:, c, :], Wo[:, c, :], gate_bc[:, :], ALU.mult)
                if l == 0:
                    for t in range(2):
                        for nh in range(2):
                            b = nb()
                            for c in range(4):
                                k.mm(PS[b][:, 0:512], ycT[:, c, t * 128:(t + 1) * 128],
                                     Wo[:, c, nh * 512:(nh + 1) * 512], start=(c == 0), stop=(c == 3))
                            k.tt("dve", tmpc[:, :], PS[b][:, 0:512], gate_c_bc[:, nh * 512:(nh + 1) * 512], ALU.mult)
                            k.tt("pool", HC[:, t, nh * 512:(nh + 1) * 512], HC[:, t, nh * 512:(nh + 1) * 512],
                                 tmpc[:, :], ALU.add)
                for t in range(16):
                    for nh in range(2):
                        b = nb()
                        for c in range(4):
                            k.mm(PS[b][:, 0:512], yT[:, c, t * 128:(t + 1) * 128],
                                 Wg[:, c, nh * 512:(nh + 1) * 512], start=(c == 0), stop=(c == 3))
                        if nh == 0:
                            k.tt("dve", X[:, t, nh * 512:(nh + 1) * 512], X[:, t, nh * 512:(nh + 1) * 512],
                                 PS[b][:, 0:512], ALU.add)
                        else:
                            ev = evb[t % 2]
                            k.cp("act", ev[:, :], PS[b][:, 0:512])
                            k.tt("pool", X[:, t, nh * 512:(nh + 1) * 512], X[:, t, nh * 512:(nh + 1) * 512],
                                 ev[:, :], ALU.add)
                    if half == 1:
                        k.act(sqj[:, :], X[:, t, :], AF.Square, accum_out=ssX[:, t:t + 1])

        def phase_B(l, prefetch=None):
            with ExitStack() as sB:
                WB = sb("WB", [128, 8, 512], BF16, sB)
                Wf = sb("Wf", [128, 2, 256], BF16, sB)
                cdsd = sb("cdsd", [128, 2, 512], BF16, sB)
                cs256 = sb("cs256", [128, 2, 2, 256], BF16, sB)
                fuT = sb("fuT", [128, 2, SEQ], BF16, sB)
                AB = sb("AB", [128, 16, 512], BF16, sB)
                Gf = sb("Gf", [128, 2, SEQ], BF16, sB)
                thb = sb("thbB", [128, 512], BF16, sB)
                dbuf = [sb("dbuf%d" % i, [128, 1024], BF16, sB) for i in range(4)]
                tmpR = sb("tmpR", [128, 512], F32, sB)
                load_w(WB, win_d[l], 2048, 2560)
                load_w(Wf, wfour_d[l], 0, 256, nk=2)
                if prefetch is not None:
                    prefetch()
                k.dma("sp", cdsd[:, :, :], cdsd_d[:, :, :])
                k.dma("sp", cs256[:, :, :, :], cs256_d[:, :, :, :])

                def four(n, hsrc, ydst):
                    T = n // 128
                    gs = min(512, n)
                    NG = n // gs
                    for cc in range(2):
                        for g in range(NG):
                            ps = proj_fm(WB, cc * 128, hsrc, g * gs, gs)
                            k.cp("act", fuT[:, cc, g * gs:(g + 1) * gs], ps)
                    for cc in range(2):
                        for g in range(NG):
                            ps = proj_fm(WB, 256 + cc * 128, hsrc, g * gs, gs)
                            gate_chunk(ps, Gf[:, cc, g * gs:(g + 1) * gs], thb[:, 0:gs])
                    for t in range(T):
                        b = nb()
                        for cc in range(2):
                            k.mm(PS[b][:, 0:512], fuT[:, cc, t * 128:(t + 1) * 128], cdsd[:, cc, :],
                                 start=(cc == 0), stop=(cc == 1))
                        k.cp("act" if t % 2 else "dve", AB[:, t, :], PS[b][:, 0:512])
                    if n == SEQ:
                        i = 0
                        for m in range(2):
                            for kt in range(16):
                                buf = dbuf[i % 4]
                                i += 1
                                k.dma("sp", buf[:, :], cs2048_d[m, kt * 128:(kt + 1) * 128, 0:1024])
                                for cc2 in range(2):
                                    for h in range(2):
                                        k.mm(PS[m * 4 + cc2 * 2 + h][:, 0:512],
                                             AB[:, kt, m * 256 + cc2 * 128:m * 256 + (cc2 + 1) * 128],
                                             buf[:, h * 512:(h + 1) * 512],
                                             start=(kt == 0), stop=(kt == 15))
                        for cc2 in range(2):
                            for h in range(2):
                                pP = PS[cc2 * 2 + h][:, 0:512]
                                pR = PS[4 + cc2 * 2 + h][:, 0:512]
                                k.cp("act", tmpR[:, :], pR)
                                k.tt("dve", fuT[:, cc2, h * 512:(h + 1) * 512], pP, tmpR[:, :], ALU.add)
                                lo = 1 if h == 0 else 0
                                cntm = 512 - lo
                                fwd = fuT[:, cc2, SEQ - (h + 1) * 512 + 1:SEQ - h * 512 - lo + 1]
                                rev = bass.AP(fwd.tensor, fwd.offset + cntm - 1, [list(fwd.ap[0]), [-1, cntm]])
                                S.op("dve", (lambda rev=rev, pP=pP, lo=lo: nc.vector.tensor_tensor(
                                    rev, pP[:, lo:512], tmpR[:, lo:512], ALU.subtract)),
                                    reads=[pP, tmpR[:, :]], writes=[fwd])
                        bq = nb()
                        for cc2 in range(2):
                            for kt in range(16):
                                k.mm(PS[bq][:, cc2:cc2 + 1], AB[:, kt, cc2 * 128:(cc2 + 1) * 128],
                                     CB[:, CB_NYQ:CB_NYQ + 1], start=(kt == 0), stop=(kt == 15))
                        for cc2 in range(2):
                            k.cp("dve", fuT[:, cc2, 1024:1025], PS[bq][:, cc2:cc2 + 1])
                    else:
                        for cc2 in range(2):
                            b = nb()
                            for m in range(2):
                                for kt in range(2):
                                    k.mm(PS[b][:, 0:n], AB[:, kt, m * 256 + cc2 * 128:m * 256 + (cc2 + 1) * 128],
                                         cs256[:, m, kt, :], start=(m == 0 and kt == 0), stop=(m == 1 and kt == 1))
                            k.cp("act", fuT[:, cc2, 0:n], PS[b][:, 0:n])
                    for cc3 in range(2):
                        for g in range(NG):
                            b = nb()
                            for cc in range(2):
                                k.mm(PS[b][:, 0:gs], Wf[:, cc, cc3 * 128:(cc3 + 1) * 128],
                                     fuT[:, cc, g * gs:(g + 1) * gs], start=(cc == 0), stop=(cc == 1))
                            k.stt(ydst[:, 2 + cc3, g * gs:(g + 1) * gs], PS[b][:, 0:gs],
                                  vT[:, l, V_BF + cc3:V_BF + cc3 + 1], Gf[:, cc3, g * gs:(g + 1) * gs],
                                  ALU.add, ALU.mult)

                if l == 0:
                    four(NCTX, hcT, ycT)
                four(SEQ, hT, yT)

        def phase_C(l, WC):
            with ExitStack() as sC:
                Ub = sb("Ub", [128, 2, SEQ + 32], BF16, sC)
                acc = sb("acc", [128, 2, SEQ], F32, sC)
                Gc = sb("Gc", [128, 2, SEQ], BF16, sC)
                ybf = [sb("ybf%d" % i, [128, 2, 512], BF16, sC) for i in range(2)]
                ysq = [sb("ysq%d" % i, [128, 2, 512], BF16, sC) for i in range(2)]
                msq = [sb("msq0", [128, 512], F32, sC)] * 2
                var_t = [sb("var_t%d" % i, [128, 512], F32, sC) for i in range(4)]
                dd = [sb("dd0", [128, 512], F32, sC)] * 2
                zz = [sb("zz%d" % i, [128, 512], BF16, sC) for i in range(2)]
                thc = [sb("thc%d" % i, [128, 512], BF16, sC) for i in range(2)]
                thb = thc[0]
                dg = [sb("dg%d" % i, [128, 128], BF16, sC) for i in range(6)]
                di = [0]

                def conv(n, hsrc, ydst):
                    gs = min(512, n)
                    NG = n // gs
                    k.memset("pool", Ub[:, :, 0:15], 0.0)
                    k.memset("pool", Ub[:, :, 15 + n:30 + n], 0.0)
                    for cc in range(2):
                        for g in range(NG):
                            psa = proj_fm(WC, cc * 128, hsrc, g * gs, gs)
                            psb = proj_fm(WC, 256 + cc * 128, hsrc, g * gs, gs)
                            k.act(thb[:, 0:gs], psb, AF.Tanh, scale=0.5)
                            k.stt(Ub[:, cc, 15 + g * gs:15 + (g + 1) * gs], thb[:, 0:gs], 1.0, psa, ALU.add, ALU.mult)
                    for cc in range(2):
                        for g in range(NG):
                            ps = proj_fm(WC, 512 + cc * 128, hsrc, g * gs, gs)
                            gate_chunk(ps, Gc[:, cc, g * gs:(g + 1) * gs], thb[:, 0:gs])
                    for cc in range(2):
                        w0 = V_TAP + cc * 31
                        for kk in range(31):
                            d = dg[di[0] % 6]
                            di[0] += 1
                            k.ts("pool", d[:, :], ident, vT[:, l, w0 + kk:w0 + kk + 1], 0.0, ALU.mult, ALU.add)
                            for g in range(NG):
                                k.mm(PS[cc * NG + g][:, 0:gs], d[:, :], Ub[:, cc, kk + g * gs:kk + (g + 1) * gs],
                                     start=(kk == 0), stop=(kk == 30))
                    for cc in range(2):
                        for g in range(NG):
                            k.act(acc[:, cc, g * gs:(g + 1) * gs], PS[cc * NG + g][:, 0:gs], AF.Identity,
                                  bias=vT[:, l, V_CB + cc:V_CB + cc + 1], scale=1.0)
                    for g in range(NG):
                        sl = slice(g * gs, (g + 1) * gs)
                        yb, yq = ybf[g % 2], ysq[g % 2]
                        for cc in range(2):
                            k.cp("pool", yb[:, cc, 0:gs], acc[:, cc, sl])
                            k.act(yq[:, cc, 0:gs], acc[:, cc, sl], AF.Square)
                        bm = g
                        for cc in range(2):
                            k.mm(PS[bm][:, 0:gs], invc, yb[:, cc, 0:gs], start=(cc == 0), stop=(cc == 1))
                        be = 4 + (g % 2)
                        for cc in range(2):
                            k.mm(PS[be][:, 0:gs], invc, yq[:, cc, 0:gs], start=(cc == 0), stop=(cc == 1))
                        k.act(msq[g % 2][:, 0:gs], PS[bm][:, 0:gs], AF.Square)
                        k.tt("dve", var_t[g][:, 0:gs], PS[be][:, 0:gs], msq[g % 2][:, 0:gs], ALU.subtract)
                        k.act(var_t[g][:, 0:gs], var_t[g][:, 0:gs], AF.Sqrt, bias=EPS, scale=1.0)
                        k.recip(var_t[g][:, 0:gs], var_t[g][:, 0:gs])
                    for g in range(NG):
                        sl = slice(g * gs, (g + 1) * gs)
                        for cc in range(2):
                            d_ = dd[cc]
                            k.tt("dve", d_[:, 0:gs], acc[:, cc, sl], PS[g][:, 0:gs], ALU.subtract)
                            k.tt("dve", d_[:, 0:gs], d_[:, 0:gs], var_t[g][:, 0:gs], ALU.mult)
                            k.act(zz[cc][:, 0:gs], d_[:, 0:gs], AF.Identity, bias=vT[:, l, V_LB + cc:V_LB + cc + 1],
                                  scale=vT[:, l, V_LG + cc:V_LG + cc + 1])
                            k.act(thc[cc][:, 0:gs], d_[:, 0:gs], AF.Tanh, bias=vT[:, l, V_LB2 + cc:V_LB2 + cc + 1],
                                  scale=vT[:, l, V_LG2 + cc:V_LG2 + cc + 1])
                            k.stt(zz[cc][:, 0:gs], thc[cc][:, 0:gs], 1.0, zz[cc][:, 0:gs], ALU.add, ALU.mult)
                            k.tt("pool", ydst[:, cc, sl], zz[cc][:, 0:gs], Gc[:, cc, sl], ALU.mult)

                if l == 0:
                    conv(NCTX, hcT, ycT)
                conv(SEQ, hT, yT)

        def final():
            with ExitStack() as sf:
                ss = sb("fss", [128, 16], F32, sf)
                rstd = sb("frstd", [128, 16], F32, sf)
                junk = sb("fjunk", [128, 1024], BF16, sf)
                fg = sb("fg", [128, 1024], F32, sf)
                ot = [sb("ot%d" % i, [128, 1024], F32, sf) for i in range(4)]
                k.dma("sp", fg[:, :], fg_d[0:1, :].partition_broadcast(128))
                k.act(rstd[:, :], ssX[:, :], AF.Sqrt, bias=EPS, scale=1.0 / D_MODEL)
                k.recip(rstd[:, :], rstd[:, :])
                for t in range(16):
                    if t % 2 == 0:
                        k.stt(ot[t % 4][:, :], X[:, t, :], rstd[:, t:t + 1], fg[:, :], ALU.mult, ALU.mult)
                    else:
                        k.act(ot[t % 4][:, :], X[:, t, :], AF.Copy, scale=rstd[:, t:t + 1])
                        k.tt("pool", ot[t % 4][:, :], ot[t % 4][:, :], fg[:, :], ALU.mult)
                    out_dmas.append(k.dma("sp", out_d[t * 128:(t + 1) * 128, :], ot[t % 4][:, :]))

        ada(0)
        for l in range(nlayers):
            with ExitStack() as sw:
                if stop_at >= 3:
                    WKV, WQG = load_A_weights(l, sw)
                if stop_at >= 2:
                    phase_norm(l, HC, 2, 1, hcT)
                    phase_norm(l, X, 16, 0, hT)
                    if l == 0:
                        dump("hT", hT, [128, 8, SEQ], BF16)
                        dump("hcT", hcT, [128, 8, NCTX], BF16)
                if stop_at >= 3:
                    phase_A(l, WKV, WQG)
            with ExitStack() as sl:
                gate_bc = sb("gate_bc", [128, 1024], F32, sl)
                gate_c_bc = sb("gate_c_bc", [128, 1024], F32, sl)
                with ExitStack() as swc:
                    WC = sb("WC", [128, 8, 768], BF16, swc)
                    if stop_at >= 4:
                        gates(l, gate_bc, gate_c_bc)
                        outproj(l, 0, gate_bc, gate_c_bc,
                                prefetch=(lambda: load_w(WC, win_d[l], 1280, 2048)) if stop_at >= 5 else None)
                    if l + 1 < nlayers:
                        ada(l + 1)
                    if stop_at >= 5:
                        phase_C(l, WC)
                Wo1 = sb("Wo1", [128, 4, 1024], BF16, sl)
                if stop_at >= 6:
                    phase_B(l, prefetch=(lambda: load_w(Wo1, wout_d[l, 512:1024, :], 0, 1024, nk=4))
                            if stop_at >= 7 else None)
                    if l == 0:
                        dump("yT_BC", yT, [128, 4, SEQ], BF16)
                        dump("ycT_BC", ycT, [128, 4, NCTX], BF16)
                if stop_at >= 7:
                    outproj(l, 1, gate_bc, gate_c_bc, Wo_pre=Wo1)
                    if l == 0:
                        dump("HC1", HC, [128, 2, 1024], F32)
                        dump("X1", X, [128, 16, 1024], F32)
        final()
        S.final_wait(out_dmas)
    return nc, S


_PROG = {}


def _get_prog():
    if "nc" not in _PROG:
        _, s0 = build(None)
        nc, _ = build(s0.used)
        _PROG["nc"] = nc
    return _PROG["nc"]


def kernel(**inputs):
    maps = _host_inputs(**inputs)
    nc = _get_prog()
    res = run_bass_kernel_spmd(nc, maps, core_ids=list(range(NCORES)))
    out = np.stack([np.asarray(r["out"], dtype=np.float32) for r in res.results], axis=0)
    return out
```

```python
import math
from contextlib import ExitStack

import numpy as np
import ml_dtypes

import concourse.bass as bass
import concourse.mybir as mybir
from concourse.bass_utils import run_bass_kernel_spmd

F32 = mybir.dt.float32
BF16 = mybir.dt.bfloat16
AF = mybir.ActivationFunctionType
ALU = mybir.AluOpType
NPBF = ml_dtypes.bfloat16

_DTSIZE = {F32: 4, BF16: 2}

D_MODEL = 1024
SEQ = 2048
NCTX = 256
DEPTH = 2
EPS = 1e-6
IN_W = 2560
NCORES = 8


class Sched:
    ENGS = ("pe", "act", "dve", "pool")

    def __init__(self, nc, flagged=None):
        self.nc = nc
        self.emit = flagged is not None
        self.flagged = flagged if flagged is not None else set()
        self.used = set()
        self.nops = 0
        self.buckets = {}
        self.mloc = {}
        self.sig = {}
        self.openg = {}
        self.waited = {}
        self.waited_idx = {}
        self.sems = {}
        self.semcnt = {}
        self.dma_ring = []
        self.dma_ring_cnt = []
        self.dma_i = 0
        self.out_dmas = []
        self.stack = None
        self.BK = 2048
        self.log = None
        self.eseq = {}
        self.seq_of = {}
        self.tag = ''

    def setup_sems(self, stack, ndma=24):
        if not self.emit:
            self.ndma = ndma
            return
        self.stack = stack
        for e in self.ENGS:
            self.sems[e] = stack.enter_context(self.nc.semaphore("sig_" + e))
            self.semcnt[e] = 0
        self.rings = {}
        for q, n in (("sp", ndma), ("pool", 12)):
            sems = [stack.enter_context(self.nc.semaphore("dma_%s%d" % (q, i))) for i in range(n)]
            self.rings[q] = [sems, [0] * n, 0]
        self.ndma = ndma

    def eng(self, e):
        nc = self.nc
        return {"pe": nc.tensor, "act": nc.scalar, "dve": nc.vector, "pool": nc.gpsimd,
                "sp": nc.sync}[e]

    def region(self, ap):
        t = ap.tensor
        name = t.name
        space = str(ap.space)
        if "SB" not in space and "PSUM" not in space:
            return None
        if name not in self.mloc:
            m = self.nc.lookup_mloc(t)
            base = m.addr + (m.bank * 2048 if "PSUM" in space else 0)
            assert base % 64 == 0, (name, base)
            self.mloc[name] = base
        base = self.mloc[name]
        pairs = ap.ap
        pstep, npart = pairs[0]
        off = ap.offset
        if pstep > 0:
            p0 = off // pstep
            foff = off % pstep
        else:
            p0, foff = 0, off
        elo = ehi = 0
        for st, cnt in pairs[1:]:
            if st < 0:
                elo += (cnt - 1) * st
            else:
                ehi += (cnt - 1) * st
        sz = _DTSIZE[ap.dtype]
        lo = (base + (foff + elo) * sz) // 64 * 64
        hi = -(-(base + (foff + ehi + 1) * sz) // 64) * 64
        if "PSUM" in space:
            return ("P", 0, 128, lo // 2048 * 2048, -(-hi // 2048) * 2048)
        return ("S", p0, p0 + npart, lo, hi)

    def _scan(self, reg, want_reads):
        sp, p0, p1, lo, hi = reg
        deps = set()
        for b in range(lo // self.BK, (hi - 1) // self.BK + 1):
            for rec in self.buckets.get((sp, b), ()):
                rp0, rp1, rlo, rhi, idx, isw = rec
                if rlo < hi and lo < rhi and rp0 < p1 and p0 < rp1:
                    if isw or want_reads:
                        deps.add(idx)
        return deps

    def _insert(self, reg, idx, isw, engkey):
        sp, p0, p1, lo, hi = reg
        for b in range(lo // self.BK, (hi - 1) // self.BK + 1):
            lst = self.buckets.setdefault((sp, b), [])
            blo, bhi = max(lo, b * self.BK), min(hi, (b + 1) * self.BK)
            new = []
            for rec in lst:
                rp0, rp1, rlo, rhi, ridx, risw = rec
                cl, ch = max(rlo, b * self.BK), min(rhi, (b + 1) * self.BK)
                covered = (p0 <= rp0 and rp1 <= p1 and blo <= cl and ch <= bhi)
                if isw and covered:
                    continue
                if (not isw) and (not risw) and covered and self.openg.get(ridx) == engkey \
                        and engkey in self.ENGS:
                    continue
                new.append(rec)
            new.append((p0, p1, lo, hi, idx, isw))
            self.buckets[(sp, b)] = new

    def op(self, eng, fn, reads=(), writes=(), dma=False, extra_deps=()):
        idx = self.nops
        self.nops += 1
        engkey = ("dma%d" % idx) if dma else eng
        self.openg[idx] = engkey
        rregs = [r for r in (self.region(a) for a in reads) if r is not None]
        wregs = [r for r in (self.region(a) for a in writes) if r is not None]
        wregs += [r for r in rregs if r[0] == "P"]
        rregs = [r for r in rregs if r[0] != "P"]
        deps = set(extra_deps)
        for r in rregs:
            deps |= self._scan(r, False)
        for w in wregs:
            deps |= self._scan(w, True)
        self.eseq[eng] = self.eseq.get(eng, 0) + 1
        self.seq_of[idx] = self.eseq[eng]
        fdeps = set()
        for d in deps:
            de = self.openg[d]
            if de == engkey:
                if eng == "pe":
                    continue
                if eng in ("act", "dve") and self.seq_of[idx] - self.seq_of[d] >= 3:
                    continue
            fdeps.add(d)
        for w in wregs:
            self._insert(w, idx, True, engkey)
        for r in rregs:
            self._insert(r, idx, False, engkey)
        wi = self.waited_idx.setdefault(eng, {})
        latest = {}
        for d in fdeps:
            ek = self.openg[d]
            if latest.get(ek, -1) < d:
                latest[ek] = d
        needed = []
        for ek, d in latest.items():
            if ek.startswith("dma") or wi.get(ek, -1) < d:
                needed.append(d)
                wi[ek] = d
        self.used |= set(needed)
        if self.log is not None:
            self.log.append((idx, eng, self.tag, sorted(needed)))
        if not self.emit:
            return idx
        E = self.eng(eng)
        wt = self.waited.setdefault(eng, {})
        need = {}
        for d in needed:
            sem, val = self.sig[d]
            k = id(sem)
            if need.get(k, (None, -1))[1] < val:
                need[k] = (sem, val)
        if dma:
            ring = self.rings[eng]
            slot = ring[2] % len(ring[0])
            ring[2] += 1
            dsem = ring[0][slot]
            prev = ring[1][slot]
            if prev > 0:
                k = id(dsem)
                if need.get(k, (None, -1))[1] < prev:
                    need[k] = (dsem, prev)
        for k, (wsem, wval) in need.items():
            if wt.get(k, -1) >= wval:
                continue
            E.wait_ge(wsem, wval)
            wt[k] = wval
        inst = fn()
        if dma:
            ring[1][slot] = prev + 16
            inst.then_inc(dsem, 16)
            self.sig[idx] = (dsem, prev + 16)
        elif idx in self.flagged:
            self.semcnt[eng] += 1
            inst.then_inc(self.sems[eng], 1)
            self.sig[idx] = (self.sems[eng], self.semcnt[eng])
        return idx

    def final_wait(self, idxs):
        self.used |= set(idxs)
        if not self.emit:
            return
        for d in idxs:
            sem, val = self.sig[d]
            self.nc.sync.wait_ge(sem, val)


class K:
    def __init__(self, nc, S):
        self.nc = nc
        self.S = S
        self.rr = 0

    def mm(self, out, lhsT, rhs, start=True, stop=True):
        nc = self.nc
        return self.S.op("pe", lambda: nc.tensor.matmul(out, lhsT, rhs, start=start, stop=stop),
                         reads=[lhsT, rhs], writes=[out])

    def tr(self, out, in_, ident):
        nc = self.nc
        return self.S.op("pe", lambda: nc.tensor.transpose(out, in_, ident),
                         reads=[in_, ident], writes=[out])

    def act(self, out, in_, func, bias=None, scale=None, accum_out=None):
        nc = self.nc
        kw = {}
        reads = [in_]
        writes = [out]
        if bias is not None:
            kw["bias"] = bias
            if not isinstance(bias, (int, float)):
                reads.append(bias)
        if scale is not None:
            kw["scale"] = scale
            if not isinstance(scale, (int, float)):
                reads.append(scale)
        if accum_out is not None:
            kw["accum_out"] = accum_out
            writes.append(accum_out)
        return self.S.op("act", lambda: nc.scalar.activation(out, in_, func, **kw),
                         reads=reads, writes=writes)

    def _v(self, eng):
        return self.nc.vector if eng == "dve" else self.nc.gpsimd

    def tt(self, eng, out, in0, in1, op):
        v = self._v(eng)
        return self.S.op(eng, lambda: v.tensor_tensor(out, in0, in1, op), reads=[in0, in1],
                         writes=[out])

    def ts(self, eng, out, in0, s1, s2=None, op0=ALU.mult, op1=None):
        v = self._v(eng)
        reads = [in0]
        if not isinstance(s1, (int, float)):
            reads.append(s1)
        if s2 is not None and not isinstance(s2, (int, float)):
            reads.append(s2)
        if op1 is None:
            return self.S.op(eng, lambda: v.tensor_scalar(out, in0, s1, None, op0), reads=reads,
                             writes=[out])
        return self.S.op(eng, lambda: v.tensor_scalar(out, in0, s1, s2, op0, op1), reads=reads,
                         writes=[out])

    def stt(self, out, in0, scalar, in1, op0, op1):
        v = self.nc.vector
        reads = [in0, in1]
        if not isinstance(scalar, (int, float)):
            reads.append(scalar)
        return self.S.op("dve", lambda: v.scalar_tensor_tensor(out, in0, scalar, in1, op0, op1),
                         reads=reads, writes=[out])

    def cp(self, eng, out, in_):
        if eng == "act":
            nc = self.nc
            return self.S.op("act", lambda: nc.scalar.copy(out, in_), reads=[in_], writes=[out])
        v = self._v(eng)
        return self.S.op(eng, lambda: v.tensor_copy(out, in_), reads=[in_], writes=[out])

    def recip(self, out, in_):
        v = self.nc.vector
        return self.S.op("dve", lambda: v.reciprocal(out, in_), reads=[in_], writes=[out])

    def memset(self, eng, out, val):
        v = self._v(eng)
        return self.S.op(eng, lambda: v.memset(out, val), reads=[], writes=[out])

    def dma(self, q, out, in_):
        E = self.S.eng(q)
        return self.S.op(q, lambda: E.dma_start(out=out, in_=in_), reads=[in_], writes=[out],
                         dma=True)


VR = 94
V_SHIFT, V_SCALE, V_NG, V_TAP, V_CB, V_LG, V_LB, V_BF, V_LG2, V_LB2 = 0, 8, 16, 24, 86, 88, 90, 92, 96, 98
CB_ID, CB_PROT, CB_ONES, CB_INVC, CB_SEL0, CB_SEL1, CB_NYQ = 0, 128, 256, 384, 512, 640, 768
CBW = 832
QPERM = np.array([(j * 4 + c) * 64 + d for c in range(4) for j in range(2) for d in range(64)])


def _consts():
    cb = np.zeros((128, CBW), np.float32)
    cb[:, CB_ID:CB_ID + 128] = np.eye(128)
    prot = np.zeros((128, 128), np.float32)
    for m in range(128):
        d = m % 64
        partner = d + 16 if (d % 32) < 16 else d - 16
        prot[(m // 64) * 64 + partner, m] = 1.0
    cb[:, CB_PROT:CB_PROT + 128] = prot
    cb[:, CB_ONES:CB_ONES + 128] = 1.0
    cb[:, CB_INVC:CB_INVC + 128] = 1.0 / 256.0
    cb[64:66, CB_SEL0:CB_SEL0 + 64] = 2.0
    cb[0:2, CB_SEL1 + 64:CB_SEL1 + 128] = 2.0
    cb[:, CB_NYQ] = ((-1.0) ** np.arange(128)) / np.sqrt(SEQ)
    kk = np.arange(128)[:, None]
    qq = np.arange(128)[None, :]
    mprev = np.where(qq <= kk, 0.0, -30000.0).astype(np.float32)
    mnext = np.where(kk <= qq, 0.0, -30000.0).astype(np.float32)
    masks = np.concatenate([np.tile(mprev, (1, 4)), np.tile(mnext, (1, 4))], axis=1)
    t = np.arange(SEQ)
    row = (t // 64).astype(np.float32)
    col = (t % 64).astype(np.float32)
    freqs = (np.float32(10000.0) ** (-np.arange(16, dtype=np.float32) / np.float32(16))).astype(np.float32)
    rope = np.zeros((128, 2, SEQ), np.float32)
    for p in range(128):
        d = p % 64
        pos = row if d < 32 else col
        i = d % 16
        ang = (pos * freqs[i]).astype(np.float32)
        rope[p, 0] = np.cos(ang)
        sgn = -1.0 if (d % 32) < 16 else 1.0
        rope[p, 1] = sgn * np.sin(ang)
    ch = np.arange(256)
    same = (ch[:, None] // 64) == (ch[None, :] // 64)
    angd = 2.0 * np.pi * ((ch[:, None] % 64) * (ch[None, :] % 64) % 64) / 64.0
    cd = np.where(same, np.cos(angd), 0.0) / 16.0
    sd = np.where(same, np.sin(angd), 0.0) / 16.0
    cdsd = np.concatenate([cd, sd], axis=1).reshape(2, 128, 512).transpose(1, 0, 2)

    def cs(n):
        a = np.arange(n)
        ang = 2.0 * np.pi * ((a[:, None] * a[None, :]) % n) / n
        return np.stack([np.cos(ang), -np.sin(ang)]) / np.sqrt(n)
    cs256 = cs(256).reshape(2, 2, 128, 256).transpose(2, 0, 1, 3)
    cs2048 = cs(2048)
    return dict(cb=cb.astype(NPBF), idf=np.eye(128, dtype=np.float32), masks=masks.astype(NPBF),
                rope=rope.astype(NPBF), cdsd=np.ascontiguousarray(cdsd).astype(NPBF),
                cs256=np.ascontiguousarray(cs256).astype(NPBF), cs2048=cs2048.astype(NPBF))


_CONST_CACHE = {}


def _get_consts():
    if not _CONST_CACHE:
        _CONST_CACHE.update(_consts())
    return _CONST_CACHE


def _host_inputs(x, c, ctx, c_ctx, w_ada, b_ada, norm_g, w_in, attn_sink, conv_w, conv_b,
                 conv_ln_g, conv_ln_b, w_four, b_four, w_out, final_g):
    f = np.float32
    w_in = np.asarray(w_in, f)
    w_out = np.asarray(w_out, f)
    w_in_p = np.concatenate([w_in[:, :, 0:512][:, :, QPERM], w_in[:, :, 512:768],
                             w_in[:, :, 768:1280][:, :, QPERM], w_in[:, :, 1280:]], axis=2)
    w_out_p = np.concatenate([w_out[:, 0:512][:, QPERM], w_out[:, 512:]], axis=1)
    vecs = np.zeros((DEPTH, VR, 128), f)
    for l in range(DEPTH):
        vecs[l, 0:16] = np.asarray(b_ada, f)[l, 0:2048].reshape(16, 128)
        vecs[l, 16:24] = np.asarray(norm_g, f)[l].reshape(8, 128)
        cw = np.asarray(conv_w, f)[l]
        vecs[l, 24:55] = cw[:, 0:128]
        vecs[l, 55:86] = cw[:, 128:256]
        vecs[l, 86:88] = np.asarray(conv_b, f)[l].reshape(2, 128)
        vecs[l, 88:90] = np.asarray(conv_ln_g, f)[l].reshape(2, 128)
        vecs[l, 90:92] = np.asarray(conv_ln_b, f)[l].reshape(2, 128)
        vecs[l, 92:94] = np.asarray(b_four, f)[l].reshape(2, 128)
    shared = dict(vecs=vecs, bgate=np.ascontiguousarray(np.asarray(b_ada, f)[:, 2048:3072]),
                  final_g=np.asarray(final_g, f).reshape(1, 1024),
                  sinkrep=np.ascontiguousarray(np.repeat(np.asarray(attn_sink, f).reshape(16), 128).reshape(1, 2048)),
                  w_ada=np.ascontiguousarray(np.asarray(w_ada, f)), w_in=np.ascontiguousarray(w_in_p),
                  w_out=np.ascontiguousarray(w_out_p), w_four=np.ascontiguousarray(np.asarray(w_four, f)))
    shared.update(_get_consts())
    maps = []
    cc = np.asarray(c_ctx, f).reshape(8, 128)
    for b in range(NCORES):
        cvec = np.zeros((16, 128), f)
        cvec[0::2] = np.asarray(c, f)[b].reshape(8, 128)
        cvec[1::2] = cc
        m = dict(shared)
        m["x"] = np.ascontiguousarray(np.asarray(x, f)[b])
        m["ctx"] = np.ascontiguousarray(np.asarray(ctx, f)[b])
        m["cvec"] = cvec
        maps.append(m)
    return maps


def build(flagged, dbg=(), nlayers=DEPTH, stop_at=99):
    nc = bass.Bass("TRN2", target_bir_lowering=False)
    S = Sched(nc, flagged)
    k = K(nc, S)

    def dt(name, shape, dtype, kind="ExternalInput"):
        return nc.dram_tensor(name, shape, dtype, kind=kind).ap()

    x_d = dt("x", [SEQ, D_MODEL], F32)
    ctx_d = dt("ctx", [NCTX, D_MODEL], F32)
    cvec_d = dt("cvec", [16, 128], F32)
    vecs_d = dt("vecs", [DEPTH, VR, 128], F32)
    bgate_d = dt("bgate", [DEPTH, 1024], F32)
    fg_d = dt("final_g", [1, 1024], F32)
    sink_d = dt("sinkrep", [1, 2048], F32)
    wada_d = dt("w_ada", [DEPTH, 1024, 3072], F32)
    win_d = dt("w_in", [DEPTH, 1024, IN_W], F32)
    wout_d = dt("w_out", [DEPTH, 1024, 1024], F32)
    wfour_d = dt("w_four", [DEPTH, 256, 256], F32)
    cb_d = dt("cb", [128, CBW], BF16)
    idf_d = dt("idf", [128, 128], F32)
    masks_d = dt("masks", [128, 1024], BF16)
    rope_d = dt("rope", [128, 2, SEQ], BF16)
    cdsd_d = dt("cdsd", [128, 2, 512], BF16)
    cs256_d = dt("cs256", [128, 2, 2, 256], BF16)
    cs2048_d = dt("cs2048", [2, SEQ, SEQ], BF16)
    out_d = dt("out", [SEQ, D_MODEL], F32, kind="ExternalOutput")
    out_dmas = []

    with ExitStack() as st:
        S.setup_sems(st)

        uniq = [0]

        def sb(name, shape, dtype, stack=st):
            uniq[0] += 1
            return stack.enter_context(nc.sbuf_tensor("s%d_%s" % (uniq[0], name), shape, dtype))

        PS = [st.enter_context(nc.psum_tensor("ps%d" % i, [128, 512], F32)) for i in range(8)]
        bank = [0]

        def nb():
            b = bank[0]
            bank[0] = (b + 1) % 8
            return b

        dumped = {}

        def dump(name, t, shape, dtype):
            if name in dbg:
                dumped[name] = dumped.get(name, 0) + 1
                suffix = "" if dumped[name] == 1 else "_L%d" % (dumped[name] - 1)
                d = dt("dbg_" + name + suffix, list(shape), dtype, kind="ExternalOutput")
                idx = [slice(None)] * len(shape)
                out_dmas.append(k.dma("sp", d[tuple(idx)], t[tuple(idx)]))

        X = sb("X", [128, 16, 1024], F32)
        HC = sb("HC", [128, 2, 1024], F32)
        hT = sb("hT", [128, 8, SEQ], BF16)
        hcT = sb("hcT", [128, 8, NCTX], BF16)
        yT = sb("yT", [128, 4, SEQ], BF16)
        ycT = sb("ycT", [128, 4, NCTX], BF16)
        CB = sb("CB", [128, CBW], BF16)
        vT = sb("vT", [128, DEPTH, 112], F32)
        scf = sb("scf", [128, 16], F32)
        scT = sb("scT", [128, 16, 2], BF16)
        ssX = sb("ssX", [128, 16], F32)
        ABm = sb("ABm", [128, DEPTH, 2, 2, 8], F32)

        ident = CB[:, CB_ID:CB_ID + 128]
        prot = CB[:, CB_PROT:CB_PROT + 128]
        ones = CB[:, CB_ONES:CB_ONES + 128]
        invc = CB[:, CB_INVC:CB_INVC + 128]

        k.dma("sp", CB[:, :], cb_d[:, :])
        xv = x_d.rearrange("(t p) d -> p t d", p=128)
        for i in range(4):
            k.dma("sp", X[:, 4 * i:4 * i + 4, :], xv[:, 4 * i:4 * i + 4, :])
        k.dma("sp", HC[:, :, :], ctx_d.rearrange("(t p) d -> p t d", p=128))
        with ExitStack() as s0:
            idf = sb("idf", [128, 128], F32, s0)
            vraw = sb("vraw", [128, 128], F32, s0)
            craw = sb("craw", [16, 128], F32, s0)
            cT = sb("cT", [128, 16], F32, s0)
            thc = sb("thc", [128, 16], F32, s0)
            sqj0 = sb("sqj0", [128, 1024], BF16, s0)
            for t in range(16):
                k.act(sqj0[:, :], X[:, t, :], AF.Square, accum_out=ssX[:, t:t + 1])
            k.dma("sp", idf[:, :], idf_d[:, :])
            k.dma("sp", craw[:, :], cvec_d[:, :])
            for l in range(DEPTH):
                k.dma("sp", vraw[0:VR, :], vecs_d[l])
                b = nb()
                k.tr(PS[b][:, 0:VR], vraw[0:VR, :], idf[0:VR, 0:VR])
                k.cp("dve", vT[:, l, 0:VR], PS[b][:, 0:VR])
                k.ts("dve", vT[:, l, V_TAP:V_TAP + 62], vT[:, l, V_TAP:V_TAP + 62], 0.5, None, ALU.mult)
                k.ts("dve", vT[:, l, V_LG:V_LG + 4], vT[:, l, V_LG:V_LG + 4], 0.25, None, ALU.mult)
                k.ts("dve", vT[:, l, V_BF:V_BF + 2], vT[:, l, V_BF:V_BF + 2], 0.5, None, ALU.mult)
                k.ts("dve", vT[:, l, V_LG2:V_LG2 + 4], vT[:, l, V_LG:V_LG + 4], 2.0, None, ALU.mult)
            b = nb()
            k.tr(PS[b][:, 0:16], craw[0:16, :], idf[0:16, 0:16])
            k.cp("dve", cT[:, :], PS[b][:, 0:16])
            k.act(thc[:, :], cT[:, :], AF.Tanh, scale=0.5)
            k.stt(scf[:, :], thc[:, :], 1.0, cT[:, :], ALU.add, ALU.mult)
            k.ts("dve", scT[:, 0:8, :], scf[:, :].rearrange("p (k s) -> p k s", s=2), 0.5, None, ALU.mult)

        def proj_fm(W, col0, hsrc, tok0, ntok, alloc=None):
            b = (alloc or nb)()
            for kk in range(8):
                k.mm(PS[b][:, 0:ntok], W[:, kk, col0:col0 + 128], hsrc[:, kk, tok0:tok0 + ntok],
                     start=(kk == 0), stop=(kk == 7))
            return PS[b][:, 0:ntok]

        def gate_chunk(ps_ap, dst_ap, tmp_ap):
            k.act(tmp_ap, ps_ap, AF.Tanh, scale=0.5)
            k.stt(dst_ap, tmp_ap, 1.0, ps_ap, ALU.add, ALU.mult)

        def load_w(dst, src_rows, c0, c1, nk=8):
            for kk in range(nk):
                k.dma("pool", dst[:, kk, :], src_rows[kk * 128:(kk + 1) * 128, c0:c1])

        def ada(l):
            with ExitStack() as sa:
                wad = [sb("wad%d" % i, [128, 8, 512], BF16, sa) for i in range(2)]
                modv = sb("modv", [128, 16, 2], F32, sa)
                tmp8 = sb("tmp8", [128, 16], F32, sa)
                bmod = nb()
                for blk in range(4):
                    wbuf = wad[blk % 2]
                    load_w(wbuf, wada_d[l], blk * 512, (blk + 1) * 512)
                    for fcl in range(4):
                        fc = blk * 4 + fcl
                        for kk in range(8):
                            k.mm(PS[bmod][:, fc * 2:fc * 2 + 2], wbuf[:, kk, fcl * 128:(fcl + 1) * 128],
                                 scT[:, kk, :], start=(kk == 0), stop=(kk == 7))
                psv = PS[bmod][:, 0:32].rearrange("p (f s) -> p f s", s=2)
                for s in range(2):
                    k.tt("dve", modv[:, :, s], psv[:, :, s], vT[:, l, 0:16], ALU.add)
                for s in range(2):
                    k.ts("dve", tmp8[:, 0:8], modv[:, 8:16, s], 1.0, None, ALU.add)
                    k.tt("dve", ABm[:, l, s, 0, :], tmp8[:, 0:8], vT[:, l, V_NG:V_NG + 8], ALU.mult)
                    k.cp("dve", ABm[:, l, s, 1, :], modv[:, 0:8, s])

        def gates(l, gate_bc, gate_c_bc):
            with ExitStack() as sa:
                wad = [sb("wadg%d" % i, [128, 8, 512], BF16, sa) for i in range(2)]
                sc_rep = sb("sc_rep", [128, 2, 8, 128], BF16, sa)
                bgb = sb("bgb", [128, 1024], F32, sa)
                k.dma("sp", bgb[:, :], bgate_d[l:l + 1, :].partition_broadcast(128))
                ns = 2 if l == 0 else 1
                for s in range(ns):
                    for kk in range(8):
                        k.ts("dve", sc_rep[:, s, kk, :], ones, scf[:, 2 * kk + s:2 * kk + s + 1], 0.5,
                             ALU.mult, ALU.mult)
                for nh in range(2):
                    wbuf = wad[nh]
                    load_w(wbuf, wada_d[l], 2048 + nh * 512, 2048 + (nh + 1) * 512)
                    for s in range(ns):
                        b = nb()
                        for kk in range(8):
                            k.mm(PS[b][:, 0:512], sc_rep[:, s, kk, :], wbuf[:, kk, :],
                                 start=(kk == 0), stop=(kk == 7))
                        dstg = gate_bc if s == 0 else gate_c_bc
                        k.tt("dve", dstg[:, nh * 512:(nh + 1) * 512], PS[b][:, 0:512],
                             bgb[:, nh * 512:(nh + 1) * 512], ALU.add)

        def phase_norm(l, Xb, T, s, dst):
            with ExitStack() as sn:
                ss = sb("ss", [128, 16], F32, sn)
                rstd = sb("rstd", [128, 16], F32, sn)
                junk = sb("junk", [128, 1024], BF16, sn)
                xn = [sb("xn%d" % i, [128, 1024], BF16, sn) for i in range(8)]
                if Xb is X:
                    ss = ssX
                else:
                    for t in range(T):
                        k.act(junk[:, :], Xb[:, t, :], AF.Square, accum_out=ss[:, t:t + 1])
                k.act(rstd[:, 0:T], ss[:, 0:T], AF.Sqrt, bias=EPS, scale=1.0 / D_MODEL)
                k.recip(rstd[:, 0:T], rstd[:, 0:T])
                G4 = min(4, T)
                for g0 in range(0, T, G4):
                    for i in range(G4):
                        t = g0 + i
                        xb = xn[t % 8]
                        if t % 2 == 0:
                            k.ts("dve", xb[:, :], Xb[:, t, :], rstd[:, t:t + 1], None, ALU.mult)
                        else:
                            k.act(xb[:, :], Xb[:, t, :], AF.Copy, scale=rstd[:, t:t + 1])
                    for c in range(8):
                        b = nb()
                        pv = PS[b][:, :].bitcast(BF16)
                        for i in range(G4):
                            k.tr(pv[:, i * 128:(i + 1) * 128], xn[(g0 + i) % 8][:, c * 128:(c + 1) * 128], ident)
                        o = dst[:, c, g0 * 128:(g0 + G4) * 128]
                        A = ABm[:, l, s, 0, c:c + 1]
                        B = ABm[:, l, s, 1, c:c + 1]
                        if c % 2 == 0:
                            k.act(o, pv[:, 0:G4 * 128], AF.Identity, bias=B, scale=A)
                        else:
                            k.ts("dve", o, pv[:, 0:G4 * 128], A, B, ALU.mult, ALU.add)

        def load_A_weights(l, stack):
            WKV = sb("WKV", [128, 8, 256], BF16, stack)
            WQG = sb("WQG", [128, 8, 1024], BF16, stack)
            load_w(WKV, win_d[l], 512, 768)
            load_w(WQG[:, :, 0:512], win_d[l], 0, 512)
            load_w(WQG[:, :, 512:1024], win_d[l], 768, 1280)
            return WKV, WQG

        def phase_A(l, WKV, WQG):
            with ExitStack() as sa:
                masks = sb("masks", [128, 1024], BF16, sa)
                rope_t = sb("rope_t", [128, 2, SEQ], BF16, sa)
                kT = sb("kT", [128, SEQ], BF16, sa)
                Vaug = sb("Vaug", [128, 16, 2, 128], BF16, sa)
                kcT = sb("kcT", [128, NCTX], BF16, sa)
                Vcaug = sb("Vcaug", [128, 2, 2, 128], BF16, sa)
                raw = sb("raw", [128, 512], BF16, sa)
                tA = sb("tA", [128, 512], BF16, sa)
                tB = sb("tB", [128, 512], BF16, sa)
                den = [sb("den%d" % j, [128, 512], BF16, sa) for j in range(2)]
                k.dma("sp", masks[:, :], masks_d[:, :])
                k.dma("sp", rope_t[:, :, :], rope_d[:, :, :])
                for V_ in (Vaug, Vcaug):
                    k.memset("pool", V_[:, :, :, :], 0.0)
                    k.memset("pool", V_[:, :, 0, 64:65], 1.0)
                    k.memset("pool", V_[:, :, 1, 0:1], 1.0)
                with ExitStack() as ss_:
                    sinkst = sb("sinkst", [128, 512], F32, ss_)
                    k.memset("dve", sinkst[:, :], 0.0)
                    for j in range(2):
                        k.memset("pool", den[j][:, :], 0.0)
                        r = 65 if j == 0 else 1
                        k.dma("sp", sinkst[r:r + 1, :], sink_d[0:1, l * 1024 + j * 512:l * 1024 + (j + 1) * 512])
                    k.act(den[0][64:66, :], sinkst[64:66, :], AF.Exp)
                    k.act(den[1][0:2, :], sinkst[0:2, :], AF.Exp)

                def v4(ap):
                    return ap.rearrange("p (a b) -> p a b", a=4)

                sbank = [0]

                def nbs():
                    b = sbank[0]
                    sbank[0] = (b + 1) % 3
                    return b

                def rope(ps_ap, dst_ap, tok0, n=512, split=False, alloc=nb):
                    k.cp("act", raw[:, 0:n], ps_ap)
                    b = alloc()
                    k.mm(PS[b][:, 0:n], prot, raw[:, 0:n])
                    k.tt("dve", tA[:, 0:n], ps_ap, rope_t[:, 0, tok0:tok0 + n], ALU.mult)
                    k.tt("dve", tB[:, 0:n], PS[b][:, 0:n], rope_t[:, 1, tok0:tok0 + n], ALU.mult)
                    if split:
                        k.tt("pool", dst_ap, v4(tA[:, 0:n]), v4(tB[:, 0:n]), ALU.add)
                    else:
                        k.tt("pool", dst_ap, tA[:, 0:n], tB[:, 0:n], ALU.add)

                with ExitStack() as skv:

                    def vproj(hsrc, t, V_):
                        b = nb()
                        for kk in range(8):
                            k.mm(PS[b][:, 0:128], hsrc[:, kk, t * 128:(t + 1) * 128], WKV[:, kk, 128:256],
                                 start=(kk == 0), stop=(kk == 7))
                        ve = "act" if t % 2 else "dve"
                        k.cp(ve, V_[:, t, 0, 0:64], PS[b][:, 0:64])
                        k.cp(ve, V_[:, t, 1, 64:128], PS[b][:, 64:128])

                    psk = proj_fm(WKV, 0, hcT, 0, NCTX)
                    k.cp("act", kcT[:, :], psk)
                    for t in range(2):
                        vproj(hcT, t, Vcaug)
                    for t in range(16):
                        vproj(hT, t, Vaug)
                    for g4 in range(4):
                        psk = proj_fm(WKV, 0, hT, g4 * 512, 512)
                        rope(psk, kT[:, g4 * 512:(g4 + 1) * 512], g4 * 512)
                dump("kT", kT, [128, SEQ], BF16)
                dump("Vaug", Vaug, [128, 16, 2, 128], BF16)

                qT = [sb("qT%d" % i, [128, 4, 4, 128], BF16, sa) for i in range(2)]
                G = [sb("G%d" % i, [128, 4, 4, 128], BF16, sa) for i in range(2)]
                thb = sb("thb", [128, 512], BF16, sa)
                P = [sb("P%d" % i, [128, 5, 512], BF16, sa) for i in range(2)]
                rb = sb("rb", [128, 512], F32, sa)
                tq = sb("tq", [128, 512], BF16, sa)
                cnt = [0]
                csrc = [(kcT, kb * 128, Vcaug, kb, None) for kb in range(2)]
                mprev = masks[:, 0:512]
                mnext = masks[:, 512:1024]

                def qg_chunks(buf, hsrc, tok0, n, do_rope):
                    nq = n // 128
                    chunks = []
                    for c in range(4):
                        def fq(c=c):
                            ps = proj_fm(WQG, c * 128, hsrc, tok0, n, nbs)
                            if do_rope:
                                rope(ps, qT[buf][:, :, c, :], tok0, n, split=True, alloc=nbs)
                            else:
                                k.cp("act", qT[buf][:, 0:nq, c, :], ps.rearrange("p (a b) -> p a b", a=nq))
                        chunks.append(fq)
                    for c in range(4):
                        def fg(c=c):
                            ps = proj_fm(WQG, 512 + c * 128, hsrc, tok0, n, nbs)
                            k.act(thb[:, 0:n], ps, AF.Tanh, scale=0.5)
                            k.stt(G[buf][:, 0:nq, c, :], thb[:, 0:n].rearrange("p (a b) -> p a b", a=nq), 1.0,
                                  ps.rearrange("p (a b) -> p a b", a=nq), ALU.add, ALU.mult)
                        chunks.append(fg)
                    return chunks

                def qg(buf, hsrc, tok0, n, do_rope):
                    for f in qg_chunks(buf, hsrc, tok0, n, do_rope):
                        f()

                units = []
                if l == 0:
                    for qb in range(2):
                        for j in range(2):
                            units.append((1, qb, j, csrc, ycT, qb * 128, None))
                for g4 in range(4):
                    for qi in range(4):
                        qb = g4 * 4 + qi
                        for j in range(2):
                            src = []
                            if qb > 0:
                                src.append((kT, (qb - 1) * 128, Vaug, qb - 1, mprev))
                            src.append((kT, qb * 128, Vaug, qb, None))
                            if qb < 15:
                                src.append((kT, (qb + 1) * 128, Vaug, qb + 1, mnext))
                            units.append((g4 % 2, qi, j, src + csrc, yT, qb * 128, g4 if (qi == 0 and j == 0) else None))
                state = {}

                def S1(u):
                    buf, qi, j, sources, dst, dtok, _ = units[u]
                    r0, r1 = j * 64, (j + 1) * 64
                    pb = P[u % 2]
                    for si, (kt, koff, va, vt, mask) in enumerate(sources):
                        b = nbs()
                        k.mm(PS[b][:, 0:512], kt[r0:r1, koff:koff + 128], qT[buf][r0:r1, qi, :, :],
                             start=True, stop=(mask is None))
                        if mask is not None:
                            k.mm(PS[b][:, 0:512], ident, mask, start=False, stop=True)
                        k.act(pb[:, si, :], PS[b][:, 0:512], AF.Exp, scale=0.125)

                def S2(u):
                    buf, qi, j, sources, dst, dtok, _ = units[u]
                    pb = P[u % 2]
                    bo = 3 + 2 * j + ((u // 2) % 2)
                    n = len(sources)
                    for si, (kt, koff, va, vt, mask) in enumerate(sources):
                        k.mm(PS[bo][:, 0:512], va[:, vt, j, :], pb[:, si, :], start=(si == 0), stop=(si == n - 1))
                    dp = 64 if j == 0 else 0
                    k.cp("act", den[j][dp:dp + 1, :], PS[bo][dp:dp + 1, 0:512])
                    state[u] = bo

                def S3(u0, u1):
                    bb = 7
                    k.mm(PS[bb][:, 0:512], CB[:, CB_SEL0:CB_SEL0 + 128], den[0][:, :], start=True, stop=False)
                    k.mm(PS[bb][:, 0:512], CB[:, CB_SEL1:CB_SEL1 + 128], den[1][:, :], start=False, stop=True)
                    k.recip(rb[:, :], PS[bb][:, 0:512])
                    for u in (u0, u1):
                        buf, qi, j, sources, dst, dtok, _ = units[u]
                        r0, r1 = j * 64, (j + 1) * 64
                        k.tt("dve", tq[r0:r1, :], PS[state[u]][r0:r1, 0:512], rb[r0:r1, :], ALU.mult)
                        k.tt("pool", dst[r0:r1, :, dtok:dtok + 128], v4(tq[r0:r1, :]), G[buf][r0:r1, qi, :, :], ALU.mult)

                if l == 0:
                    qg(1, hcT, 0, NCTX, False)
                qg(0, hT, 0, 512, True)
                S1(0)
                pending = []
                for u in range(len(units)):
                    g_start = units[u][6]
                    if g_start is not None and g_start + 1 < 4:
                        for f in pending:
                            f()
                        pending = qg_chunks((g_start + 1) % 2, hT, (g_start + 1) * 512, 512, True)
                    if u + 1 < len(units):
                        if units[u + 1][6] is not None:
                            for f in pending:
                                f()
                            pending = []
                        S1(u + 1)
                    S2(u)
                    if pending:
                        pending.pop(0)()
                    if units[u][2] == 1:
                        S3(u - 1, u)
                    if l == 0 and u == 3:
                        dump("ycT_A", ycT, [128, 4, NCTX], BF16)
                dump("yT_A", yT, [128, 4, SEQ], BF16)


        def outproj(l, half, gate_bc, gate_c_bc, prefetch=None, Wo_pre=None):
            with ExitStack() as so:
                Wo = Wo_pre if Wo_pre is not None else sb("Wo", [128, 4, 1024], BF16, so)
                Wg = sb("Wg", [128, 4, 1024], BF16, so)
                tmpc = sb("tmpc", [128, 512], F32, so)
                sqj = sb("sqj", [128, 1024], BF16, so)
                if Wo_pre is None:
                    load_w(Wo, wout_d[l, half * 512:(half + 1) * 512, :], 0, 1024, nk=4)
                if prefetch is not None:
                    prefetch()
                for c in range(4):
                    k.tt("dve" if c % 2 else "pool", Wg[:, c, :], Wo[:, c, :], gate_bc[:, :], ALU.mult)
                if l == 0:
                    for t in range(2):
                        for nh in range(2):
                            b = nb()
                            for c in range(4):
                                k.mm(PS[b][:, 0:512], ycT[:, c, t * 128:(t + 1) * 128],
                                     Wo[:, c, nh * 512:(nh + 1) * 512], start=(c == 0), stop=(c == 3))
                            k.tt("dve", tmpc[:, :], PS[b][:, 0:512], gate_c_bc[:, nh * 512:(nh + 1) * 512], ALU.mult)
                            k.tt("pool", HC[:, t, nh * 512:(nh + 1) * 512], HC[:, t, nh * 512:(nh + 1) * 512],
                                 tmpc[:, :], ALU.add)
                for t in range(16):
                    for nh in range(2):
                        b = nb()
                        for c in range(4):
                            k.mm(PS[b][:, 0:512], yT[:, c, t * 128:(t + 1) * 128],
                                 Wg[:, c, nh * 512:(nh + 1) * 512], start=(c == 0), stop=(c == 3))
                        k.tt("dve", X[:, t, nh * 512:(nh + 1) * 512], X[:, t, nh * 512:(nh + 1) * 512],
                             PS[b][:, 0:512], ALU.add)
                    if half == 1:
                        k.act(sqj[:, :], X[:, t, :], AF.Square, accum_out=ssX[:, t:t + 1])

        def phase_B(l, prefetch=None):
            with ExitStack() as sB:
                WB = sb("WB", [128, 8, 512], BF16, sB)
                Wf = sb("Wf", [128, 2, 256], BF16, sB)
                cdsd = sb("cdsd", [128, 2, 512], BF16, sB)
                cs256 = sb("cs256", [128, 2, 2, 256], BF16, sB)
                fuT = sb("fuT", [128, 2, SEQ], BF16, sB)
                AB = sb("AB", [128, 16, 512], BF16, sB)
                Gf = sb("Gf", [128, 2, SEQ], BF16, sB)
                thb = sb("thbB", [128, 512], BF16, sB)
                dbuf = [sb("dbuf%d" % i, [128, 1024], BF16, sB) for i in range(4)]
                tmpR = sb("tmpR", [128, 512], F32, sB)
                load_w(WB, win_d[l], 2048, 2560)
                load_w(Wf, wfour_d[l], 0, 256, nk=2)
                if prefetch is not None:
                    prefetch()
                k.dma("sp", cdsd[:, :, :], cdsd_d[:, :, :])
                k.dma("sp", cs256[:, :, :, :], cs256_d[:, :, :, :])

                def four(n, hsrc, ydst):
                    T = n // 128
                    gs = min(512, n)
                    NG = n // gs
                    for cc in range(2):
                        for g in range(NG):
                            ps = proj_fm(WB, cc * 128, hsrc, g * gs, gs)
                            k.cp("act", fuT[:, cc, g * gs:(g + 1) * gs], ps)
                    for cc in range(2):
                        for g in range(NG):
                            ps = proj_fm(WB, 256 + cc * 128, hsrc, g * gs, gs)
                            gate_chunk(ps, Gf[:, cc, g * gs:(g + 1) * gs], thb[:, 0:gs])
                    for t in range(T):
                        b = nb()
                        for cc in range(2):
                            k.mm(PS[b][:, 0:512], fuT[:, cc, t * 128:(t + 1) * 128], cdsd[:, cc, :],
                                 start=(cc == 0), stop=(cc == 1))
                        k.cp("act" if t % 2 else "dve", AB[:, t, :], PS[b][:, 0:512])
                    if n == SEQ:
                        i = 0
                        for m in range(2):
                            for kt in range(16):
                                buf = dbuf[i % 4]
                                i += 1
                                k.dma("sp", buf[:, :], cs2048_d[m, kt * 128:(kt + 1) * 128, 0:1024])
                                for cc2 in range(2):
                                    for h in range(2):
                                        k.mm(PS[m * 4 + cc2 * 2 + h][:, 0:512],
                                             AB[:, kt, m * 256 + cc2 * 128:m * 256 + (cc2 + 1) * 128],
                                             buf[:, h * 512:(h + 1) * 512],
                                             start=(kt == 0), stop=(kt == 15))
                        for cc2 in range(2):
                            for h in range(2):
                                pP = PS[cc2 * 2 + h][:, 0:512]
                                pR = PS[4 + cc2 * 2 + h][:, 0:512]
                                k.cp("act", tmpR[:, :], pR)
                                k.tt("dve", fuT[:, cc2, h * 512:(h + 1) * 512], pP, tmpR[:, :], ALU.add)
                                lo = 1 if h == 0 else 0
                                cntm = 512 - lo
                                fwd = fuT[:, cc2, SEQ - (h + 1) * 512 + 1:SEQ - h * 512 - lo + 1]
                                rev = bass.AP(fwd.tensor, fwd.offset + cntm - 1, [list(fwd.ap[0]), [-1, cntm]])
                                S.op("dve", (lambda rev=rev, pP=pP, lo=lo: nc.vector.tensor_tensor(
                                    rev, pP[:, lo:512], tmpR[:, lo:512], ALU.subtract)),
                                    reads=[pP, tmpR[:, :]], writes=[fwd])
                        bq = nb()
                        for cc2 in range(2):
                            for kt in range(16):
                                k.mm(PS[bq][:, cc2:cc2 + 1], AB[:, kt, cc2 * 128:(cc2 + 1) * 128],
                                     CB[:, CB_NYQ:CB_NYQ + 1], start=(kt == 0), stop=(kt == 15))
                        for cc2 in range(2):
                            k.cp("dve", fuT[:, cc2, 1024:1025], PS[bq][:, cc2:cc2 + 1])
                    else:
                        for cc2 in range(2):
                            b = nb()
                            for m in range(2):
                                for kt in range(2):
                                    k.mm(PS[b][:, 0:n], AB[:, kt, m * 256 + cc2 * 128:m * 256 + (cc2 + 1) * 128],
                                         cs256[:, m, kt, :], start=(m == 0 and kt == 0), stop=(m == 1 and kt == 1))
                            k.cp("act", fuT[:, cc2, 0:n], PS[b][:, 0:n])
                    for cc3 in range(2):
                        for g in range(NG):
                            b = nb()
                            for cc in range(2):
                                k.mm(PS[b][:, 0:gs], Wf[:, cc, cc3 * 128:(cc3 + 1) * 128],
                                     fuT[:, cc, g * gs:(g + 1) * gs], start=(cc == 0), stop=(cc == 1))
                            k.stt(ydst[:, 2 + cc3, g * gs:(g + 1) * gs], PS[b][:, 0:gs],
                                  vT[:, l, V_BF + cc3:V_BF + cc3 + 1], Gf[:, cc3, g * gs:(g + 1) * gs],
                                  ALU.add, ALU.mult)

                if l == 0:
                    four(NCTX, hcT, ycT)
                four(SEQ, hT, yT)

        def phase_C(l, WC):
            with ExitStack() as sC:
                Ub = sb("Ub", [128, 2, SEQ + 32], BF16, sC)
                acc = sb("acc", [128, 2, SEQ], F32, sC)
                Gc = sb("Gc", [128, 2, SEQ], BF16, sC)
                ybf = [sb("ybf%d" % i, [128, 2, 512], BF16, sC) for i in range(2)]
                ysq = [sb("ysq%d" % i, [128, 2, 512], BF16, sC) for i in range(2)]
                msq = [sb("msq0", [128, 512], F32, sC)] * 2
                var_t = [sb("var_t%d" % i, [128, 512], F32, sC) for i in range(4)]
                dd = [sb("dd0", [128, 512], F32, sC)] * 2
                zz = [sb("zz%d" % i, [128, 512], BF16, sC) for i in range(2)]
                thc = [sb("thc%d" % i, [128, 512], BF16, sC) for i in range(2)]
                thb = thc[0]
                dg = [sb("dg%d" % i, [128, 128], BF16, sC) for i in range(6)]
                di = [0]

                def conv(n, hsrc, ydst):
                    gs = min(512, n)
                    NG = n // gs
                    k.memset("pool", Ub[:, :, 0:15], 0.0)
                    k.memset("pool", Ub[:, :, 15 + n:30 + n], 0.0)
                    for cc in range(2):
                        for g in range(NG):
                            psa = proj_fm(WC, cc * 128, hsrc, g * gs, gs)
                            psb = proj_fm(WC, 256 + cc * 128, hsrc, g * gs, gs)
                            k.act(thb[:, 0:gs], psb, AF.Tanh, scale=0.5)
                            k.stt(Ub[:, cc, 15 + g * gs:15 + (g + 1) * gs], thb[:, 0:gs], 1.0, psa, ALU.add, ALU.mult)
                    for cc in range(2):
                        for g in range(NG):
                            ps = proj_fm(WC, 512 + cc * 128, hsrc, g * gs, gs)
                            gate_chunk(ps, Gc[:, cc, g * gs:(g + 1) * gs], thb[:, 0:gs])
                    for cc in range(2):
                        w0 = V_TAP + cc * 31
                        for kk in range(31):
                            d = dg[di[0] % 6]
                            di[0] += 1
                            k.ts("pool", d[:, :], ident, vT[:, l, w0 + kk:w0 + kk + 1], 0.0, ALU.mult, ALU.add)
                            for g in range(NG):
                                k.mm(PS[cc * NG + g][:, 0:gs], d[:, :], Ub[:, cc, kk + g * gs:kk + (g + 1) * gs],
                                     start=(kk == 0), stop=(kk == 30))
                    for cc in range(2):
                        for g in range(NG):
                            k.act(acc[:, cc, g * gs:(g + 1) * gs], PS[cc * NG + g][:, 0:gs], AF.Identity,
                                  bias=vT[:, l, V_CB + cc:V_CB + cc + 1], scale=1.0)
                    for g in range(NG):
                        sl = slice(g * gs, (g + 1) * gs)
                        yb, yq = ybf[g % 2], ysq[g % 2]
                        for cc in range(2):
                            k.cp("pool", yb[:, cc, 0:gs], acc[:, cc, sl])
                            k.act(yq[:, cc, 0:gs], acc[:, cc, sl], AF.Square)
                        bm = g
                        for cc in range(2):
                            k.mm(PS[bm][:, 0:gs], invc, yb[:, cc, 0:gs], start=(cc == 0), stop=(cc == 1))
                        be = 4 + (g % 2)
                        for cc in range(2):
                            k.mm(PS[be][:, 0:gs], invc, yq[:, cc, 0:gs], start=(cc == 0), stop=(cc == 1))
                        k.act(msq[g % 2][:, 0:gs], PS[bm][:, 0:gs], AF.Square)
                        k.tt("dve", var_t[g][:, 0:gs], PS[be][:, 0:gs], msq[g % 2][:, 0:gs], ALU.subtract)
                        k.act(var_t[g][:, 0:gs], var_t[g][:, 0:gs], AF.Sqrt, bias=EPS, scale=1.0)
                        k.recip(var_t[g][:, 0:gs], var_t[g][:, 0:gs])
                    for g in range(NG):
                        sl = slice(g * gs, (g + 1) * gs)
                        for cc in range(2):
                            d_ = dd[cc]
                            k.tt("dve", d_[:, 0:gs], acc[:, cc, sl], PS[g][:, 0:gs], ALU.subtract)
                            k.tt("dve", d_[:, 0:gs], d_[:, 0:gs], var_t[g][:, 0:gs], ALU.mult)
                            k.act(zz[cc][:, 0:gs], d_[:, 0:gs], AF.Identity, bias=vT[:, l, V_LB + cc:V_LB + cc + 1],
                                  scale=vT[:, l, V_LG + cc:V_LG + cc + 1])
                            k.act(thc[cc][:, 0:gs], d_[:, 0:gs], AF.Tanh, bias=vT[:, l, V_LB2 + cc:V_LB2 + cc + 1],
                                  scale=vT[:, l, V_LG2 + cc:V_LG2 + cc + 1])
                            k.stt(zz[cc][:, 0:gs], thc[cc][:, 0:gs], 1.0, zz[cc][:, 0:gs], ALU.add, ALU.mult)
                            k.tt("pool", ydst[:, cc, sl], zz[cc][:, 0:gs], Gc[:, cc, sl], ALU.mult)

                if l == 0:
                    conv(NCTX, hcT, ycT)
                conv(SEQ, hT, yT)

        def final():
            with ExitStack() as sf:
                ss = sb("fss", [128, 16], F32, sf)
                rstd = sb("frstd", [128, 16], F32, sf)
                junk = sb("fjunk", [128, 1024], BF16, sf)
                fg = sb("fg", [128, 1024], F32, sf)
                ot = [sb("ot%d" % i, [128, 1024], F32, sf) for i in range(4)]
                k.dma("sp", fg[:, :], fg_d[0:1, :].partition_broadcast(128))
                k.act(rstd[:, :], ssX[:, :], AF.Sqrt, bias=EPS, scale=1.0 / D_MODEL)
                k.recip(rstd[:, :], rstd[:, :])
                for t in range(16):
                    if t % 2 == 0:
                        k.stt(ot[t % 4][:, :], X[:, t, :], rstd[:, t:t + 1], fg[:, :], ALU.mult, ALU.mult)
                    else:
                        k.act(ot[t % 4][:, :], X[:, t, :], AF.Copy, scale=rstd[:, t:t + 1])
                        k.tt("pool", ot[t % 4][:, :], ot[t % 4][:, :], fg[:, :], ALU.mult)
                    out_dmas.append(k.dma("sp", out_d[t * 128:(t + 1) * 128, :], ot[t % 4][:, :]))

        ada(0)
        for l in range(nlayers):
            with ExitStack() as sw:
                if stop_at >= 3:
                    WKV, WQG = load_A_weights(l, sw)
                if stop_at >= 2:
                    phase_norm(l, HC, 2, 1, hcT)
                    phase_norm(l, X, 16, 0, hT)
                    if l == 0:
                        dump("hT", hT, [128, 8, SEQ], BF16)
                        dump("hcT", hcT, [128, 8, NCTX], BF16)
                if stop_at >= 3:
                    phase_A(l, WKV, WQG)
            with ExitStack() as sl:
                gate_bc = sb("gate_bc", [128, 1024], F32, sl)
                gate_c_bc = sb("gate_c_bc", [128, 1024], F32, sl)
                with ExitStack() as swc:
                    WC = sb("WC", [128, 8, 768], BF16, swc)
                    if stop_at >= 4:
                        gates(l, gate_bc, gate_c_bc)
                        outproj(l, 0, gate_bc, gate_c_bc,
                                prefetch=(lambda: load_w(WC, win_d[l], 1280, 2048)) if stop_at >= 5 else None)
                    if l + 1 < nlayers:
                        ada(l + 1)
                    if stop_at >= 5:
                        phase_C(l, WC)
                Wo1 = sb("Wo1", [128, 4, 1024], BF16, sl)
                if stop_at >= 6:
                    phase_B(l, prefetch=(lambda: load_w(Wo1, wout_d[l, 512:1024, :], 0, 1024, nk=4))
                            if stop_at >= 7 else None)
                    if l == 0:
                        dump("yT_BC", yT, [128, 4, SEQ], BF16)
                        dump("ycT_BC", ycT, [128, 4, NCTX], BF16)
                if stop_at >= 7:
                    outproj(l, 1, gate_bc, gate_c_bc, Wo_pre=Wo1)
                    if l == 0:
                        dump("HC1", HC, [128, 2, 1024], F32)
                        dump("X1", X, [128, 16, 1024], F32)
        final()
        S.final_wait(out_dmas)
    return nc, S


_PROG = {}


def _get_prog():
    if "nc" not in _PROG:
        _, s0 = build(None)
        nc, _ = build(s0.used)
        _PROG["nc"] = nc
    return _PROG["nc"]


def kernel(**inputs):
    maps = _host_inputs(**inputs)
    nc = _get_prog()
    res = run_bass_kernel_spmd(nc, maps, core_ids=list(range(NCORES)))
    out = np.stack([np.asarray(r["out"], dtype=np.float32) for r in res.results], axis=0)
    return out
```
